# Optimizing a Trainium2 kernel written in Bass

```python
import jax, jax.numpy as jnp
from jax import lax
import numpy as np

D_MODEL = 1024
BATCH = 8
SEQ = 8192
DEPTH = 1

D_FF = 2816
D_MIX = D_MODEL
N_HEADS = 8
HEAD_DIM = 64
N_KV_HEADS = 2
KV_GROUP = N_HEADS // N_KV_HEADS
D_ATTN = N_HEADS * HEAD_DIM
D_KV = N_KV_HEADS * HEAD_DIM
Q_BLOCK = 128
ROPE_THETA = 10000.0
AXIS_DIM = HEAD_DIM // 2
GRID_W = 64
N_SGU_GROUPS = 8
SGU_GROUP_DIM = 64
D_SGU = N_SGU_GROUPS * SGU_GROUP_DIM
CHUNK = 128
D_IN = D_ATTN + 2 * D_KV + 2 * D_SGU
EPS = 1e-6

kernel_name = "hybrid_macaron_gmlp_gqa_axial_encoder_block"


def rms_norm(x, g):
    xf = x.astype(jnp.float32)
    y = xf * lax.rsqrt(jnp.mean(xf * xf, axis=-1, keepdims=True) + EPS)
    return (y * g.astype(jnp.float32)).astype(x.dtype)


def swiglu(h, w_gate, w_up, w_down):
    return (jax.nn.silu(h @ w_gate) * (h @ w_up)) @ w_down


def axial_rope_tables(rows):
    row_idx = jnp.repeat(jnp.arange(rows, dtype=jnp.float32), GRID_W)
    col_idx = jnp.tile(jnp.arange(GRID_W, dtype=jnp.float32), rows)
    inv = 1.0 / (ROPE_THETA ** (jnp.arange(0, AXIS_DIM, 2, dtype=jnp.float32) / AXIS_DIM))
    ang = jnp.concatenate([row_idx[:, None] * inv, col_idx[:, None] * inv], axis=-1)
    return jnp.cos(ang), jnp.sin(ang)


def apply_rope(x, cos, sin):
    b, s, h, d = x.shape
    xf = x.astype(jnp.float32).reshape(b, s, h, d // 2, 2)
    x1, x2 = xf[..., 0], xf[..., 1]
    c = cos[None, :, None, :]
    sn = sin[None, :, None, :]
    out = jnp.stack([x1 * c - x2 * sn, x1 * sn + x2 * c], axis=-1)
    return out.reshape(b, s, h, d).astype(x.dtype)


def gqa_attention(q, k, v):
    b, s, _, d = q.shape
    nblk = s // Q_BLOCK
    scale = HEAD_DIM ** -0.5
    qb = q.reshape(b, nblk, Q_BLOCK, N_KV_HEADS, KV_GROUP, d).transpose(1, 0, 2, 3, 4, 5)

    def one_block(qi):
        sc = jnp.einsum('bqkgd,bskd->bkgqs', qi, k, preferred_element_type=jnp.float32) * scale
        p = jax.nn.softmax(sc, axis=-1)
        return jnp.einsum('bkgqs,bskd->bqkgd', p.astype(v.dtype), v)

    o = lax.map(one_block, qb)
    return o.transpose(1, 0, 2, 3, 4, 5).reshape(b, s, N_HEADS * d)


def spatial_gating(z, g_sgu, w_s, b_s):
    b, s, _ = z.shape
    u, vv = jnp.split(z, 2, axis=-1)
    vv = rms_norm(vv, g_sgu)
    vv = vv.reshape(b, s // CHUNK, CHUNK, N_SGU_GROUPS, SGU_GROUP_DIM)
    f = jnp.einsum('gpq,bnqgd->bnpgd', w_s, vv) + b_s.T[None, None, :, :, None]
    return u * f.reshape(b, s, D_SGU)


def setup_inputs(seed: int = 0) -> dict:
    key = jax.random.key(seed)
    ks = jax.random.split(key, 24)
    L = DEPTH
    nrm = lambda k, shape, fan_in: jax.random.normal(k, shape, jnp.float32) * fan_in ** -0.5
    gain = lambda k, shape: 1.0 + 0.02 * jax.random.normal(k, shape, jnp.float32)
    return {
        "x": jax.random.normal(ks[0], (BATCH, SEQ, D_MODEL), jnp.float32),
        "g_ffn1": gain(ks[1], (L, D_MODEL)),
        "w1_gate": nrm(ks[2], (L, D_MODEL, D_FF), D_MODEL),
        "w1_up": nrm(ks[3], (L, D_MODEL, D_FF), D_MODEL),
        "w1_down": nrm(ks[4], (L, D_FF, D_MODEL), D_FF),
        "g_mix": gain(ks[5], (L, D_MODEL)),
        "w_in": nrm(ks[6], (L, D_MODEL, D_IN), D_MODEL),
        "g_q": gain(ks[7], (L, HEAD_DIM)),
        "g_k": gain(ks[8], (L, HEAD_DIM)),
        "g_sgu": gain(ks[9], (L, D_SGU)),
        "w_s": nrm(ks[10], (L, N_SGU_GROUPS, CHUNK, CHUNK), CHUNK),
        "b_s": 1.0 + 0.02 * jax.random.normal(ks[11], (L, N_SGU_GROUPS, CHUNK), jnp.float32),
        "g_attn_out": gain(ks[12], (L, D_ATTN)),
        "g_sgu_out": gain(ks[13], (L, D_SGU)),
        "w_out": nrm(ks[14], (L, D_MIX, D_MODEL), D_MIX),
        "g_ffn2": gain(ks[15], (L, D_MODEL)),
        "w2_gate": nrm(ks[16], (L, D_MODEL, D_FF), D_MODEL),
        "w2_up": nrm(ks[17], (L, D_MODEL, D_FF), D_MODEL),
        "w2_down": nrm(ks[18], (L, D_FF, D_MODEL), D_FF),
        "g_final": gain(ks[19], (L, D_MODEL)),
    }


def reference(x, g_ffn1, w1_gate, w1_up, w1_down, g_mix, w_in, g_q, g_k, g_sgu, w_s, b_s,
              g_attn_out, g_sgu_out, w_out, g_ffn2, w2_gate, w2_up, w2_down, g_final):
    b, s, _ = x.shape
    rows = s // GRID_W
    cos, sin = axial_rope_tables(rows)
    for l in range(DEPTH):
        x = x + 0.5 * swiglu(rms_norm(x, g_ffn1[l]), w1_gate[l], w1_up[l], w1_down[l])

        h = rms_norm(x, g_mix[l])
        proj = h @ w_in[l]
        q, k, v, z = jnp.split(proj, [D_ATTN, D_ATTN + D_KV, D_ATTN + 2 * D_KV], axis=-1)

        q = rms_norm(q.reshape(b, s, N_HEADS, HEAD_DIM), g_q[l])
        k = rms_norm(k.reshape(b, s, N_KV_HEADS, HEAD_DIM), g_k[l])
        v = v.reshape(b, s, N_KV_HEADS, HEAD_DIM)
        q = apply_rope(q, cos, sin)
        k = apply_rope(k, cos, sin)
        attn = gqa_attention(q, k, v)

        sgu = spatial_gating(jax.nn.gelu(z), g_sgu[l], w_s[l], b_s[l])

        mixed = jnp.concatenate([rms_norm(attn, g_attn_out[l]), rms_norm(sgu, g_sgu_out[l])], axis=-1)
        x = x + mixed @ w_out[l]

        x = x + 0.5 * swiglu(rms_norm(x, g_ffn2[l]), w2_gate[l], w2_up[l], w2_down[l])
        x = rms_norm(x, g_final[l])
    return x
```

```python
import numpy as np
from contextlib import ExitStack
import concourse.bass as bass
import concourse.mybir as mybir
from concourse.bass_utils import run_bass_kernel_spmd

F32 = mybir.dt.float32
BF16 = mybir.dt.bfloat16
AF = mybir.ActivationFunctionType
ALU = mybir.AluOpType
AX = mybir.AxisListType

S = 8192
D = 1024
DFF = 2816
NFT = DFF // 128
DIN = 1792
TT = 512
NT = S // TT
NKT = S // 128
EPS = 1e-6
RING = 3
SLOT = 5632
GELU_C = 0.7978845608028654
STRICT_SYNC = True
BG_DELAY = 1
POST_PIECES = 275


class _Op:
    __slots__ = ("eng", "fn", "dma", "deps", "need_inc", "tick", "waits")


class Prog:
    ENGS = ("tensor", "vector", "scalar", "gpsimd", "sync")

    def __init__(self):
        self.ops = []
        self.last_w = {}
        self.readers = {}
        self.dma_last = {}
        self.dma_keys = []
        self.last_eng = {}

    def add(self, eng, fn, reads=(), writes=(), dma=None, extra_deps=()):
        op = _Op()
        op.eng = eng
        op.fn = fn
        op.dma = dma
        op.need_inc = False
        op.tick = 0
        op.waits = []
        deps = {}
        for r in reads:
            w = self.last_w.get(r)
            if w is not None:
                deps[w] = True
        for wr in writes:
            w = self.last_w.get(wr)
            if w is not None and w not in deps:
                deps[w] = False
            for rd in self.readers.get(wr, ()):
                if rd not in deps:
                    deps[rd] = False
        if dma is not None:
            if dma in self.dma_last:
                deps[self.dma_last[dma]] = True
            else:
                self.dma_keys.append(dma)
            self.dma_last[dma] = op
        for d in extra_deps:
            deps[d] = True
        op.deps = []
        for d, raw in deps.items():
            if d is op:
                continue
            if d.dma is not None or op.dma is not None:
                need = True
            elif d.eng != op.eng:
                need = True
            else:
                need = (raw or STRICT_SYNC) and op.eng != "tensor"
            if need:
                op.deps.append(d)
                d.need_inc = True
        for r in reads:
            self.readers.setdefault(r, []).append(op)
        for wr in writes:
            self.last_w[wr] = op
            self.readers[wr] = []
        self.ops.append(op)
        self.last_eng[eng] = op
        return op

    def fence(self):
        lasts = list(self.last_eng.values()) + list(self.dma_last.values())
        for e in self.ENGS:
            self.add(e, lambda eng: eng.nop(), extra_deps=[d for d in lasts])

    def finalize(self, nc, es):
        self.eng_sem = {e: es.enter_context(nc.semaphore("s_" + e)) for e in self.ENGS}
        self.dma_sem = {k: es.enter_context(nc.semaphore("d_%d" % i)) for i, k in enumerate(self.dma_keys)}
        tick = {e: 0 for e in self.ENGS}
        dcount = {k: 0 for k in self.dma_keys}
        for op in self.ops:
            if op.dma is not None:
                dcount[op.dma] += 16
                op.tick = dcount[op.dma]
            elif op.need_inc:
                tick[op.eng] += 1
                op.tick = tick[op.eng]
        waited = {e: {} for e in self.ENGS}
        for op in self.ops:
            need = {}
            for d in op.deps:
                sem = self.dma_sem[d.dma] if d.dma is not None else self.eng_sem[d.eng]
                key = id(sem)
                if key not in need or need[key][1] < d.tick:
                    need[key] = (sem, d.tick)
            for key, (sem, val) in need.items():
                if waited[op.eng].get(key, 0) >= val:
                    continue
                waited[op.eng][key] = val
                op.waits.append((sem, val))

    def replay(self, ename, eng):
        for op in self.ops:
            if op.eng != ename:
                continue
            for sem, val in op.waits:
                eng.wait_ge(sem, val)
            inst = op.fn(eng)
            if op.dma is not None:
                inst.then_inc(self.dma_sem[op.dma], 16)
            elif op.need_inc:
                inst.then_inc(self.eng_sem[ename], 1)


def build_nc(debug=False, n_tiles_a=NT, n_tiles_b=NT):
    nc = bass.Bass("TRN2", target_bir_lowering=False)
    P = Prog()

    def din(name, shape, dt=F32):
        return nc.dram_tensor(name, list(shape), dt, kind="ExternalInput").ap()

    x_d = din("x", [S, D])
    rope_d = din("rope", [S, 64])
    g_ffn1_d = din("g_ffn1", [D]); g_mix_d = din("g_mix", [D]); g_ffn2_d = din("g_ffn2", [D]); g_final_d = din("g_final", [D])
    g_q_d = din("g_q", [64]); g_k_d = din("g_k", [64]); g_sgu_d = din("g_sgu", [512])
    g_ao_d = din("g_attn_out", [512]); g_so_d = din("g_sgu_out", [512])
    ws_d = din("w_s", [8, 128, 128]); bs_d = din("b_s", [8, 128])
    wf = {}
    for nm, shp in (("w1_gate", [D, DFF]), ("w1_up", [D, DFF]), ("w1_down", [DFF, D]), ("w_in", [D, DIN]),
                    ("w_out", [D, D]), ("w2_gate", [D, DFF]), ("w2_up", [D, DFF]), ("w2_down", [DFF, D])):
        wf[nm] = din(nm, shp)
    out_d = nc.dram_tensor("out", [S, D], F32, kind="ExternalOutput").ap()
    skind = "ExternalOutput" if debug else "Internal"
    wb = {nm: nc.dram_tensor(nm + "_bf", list(wf[nm].shape), BF16, kind="Internal").ap() for nm in wf}
    x1s_d = nc.dram_tensor("x1s", [S, D], F32, kind=skind).ap()
    qts_d = nc.dram_tensor("qts", [4, 128, S], BF16, kind=skind).ap()
    if debug:
        kt_dbg = nc.dram_tensor("kt_dbg", [128, S], BF16, kind="ExternalOutput").ap()
        va_dbg = nc.dram_tensor("va_dbg", [128, NKT * 2 * 65], BF16, kind="ExternalOutput").ap()
        attn_dbg = nc.dram_tensor("attn_dbg", [S, 512], F32, kind="ExternalOutput").ap()
        x2_dbg = nc.dram_tensor("x2_dbg", [S, D], F32, kind="ExternalOutput").ap()
        x3_dbg = nc.dram_tensor("x3_dbg", [S, D], F32, kind="ExternalOutput").ap()

    def tiled(ap):
        return ap.rearrange("(n s p) d -> n p s d", s=4, p=128)

    with ExitStack() as es:
        def sb(name, shape, dt):
            return es.enter_context(nc.sbuf_tensor(name, list(shape), dt))

        ident_b = sb("ident_b", [128, 128], BF16)
        ident_f = sb("ident_f", [128, 128], F32)
        g1 = sb("g1", [128, D], F32)
        g2 = sb("g2", [128, D], F32)
        g3 = sb("g3", [128, 512], F32)
        g4 = sb("g4", [128, 512], F32)
        gq = sb("gq", [128, 64], F32)
        gk = sb("gk", [128, 64], F32)
        KT = sb("KT", [128, S], BF16)
        VA = sb("VA", [128, NKT, 2, 65], BF16)
        WsT = sb("WsT", [128, 8, 128], BF16)
        bsT = sb("bsT", [128, 8], F32)
        mhalf = sb("mhalf", [128, 16], F32)
        ring = [sb("ring%d" % i, [128, SLOT], BF16) for i in range(RING)]
        xt = [sb("xt%d" % i, [128, 4, D], F32) for i in range(2)]
        xn = [sb("xn%d" % i, [128, D], BF16) for i in range(2)]
        xnT = sb("xnT", [128, 8, TT], BF16)
        hnT = sb("hnT", [128, 8, TT], BF16)
        hT = sb("hT", [128, NFT, TT], BF16)
        sg = [sb("sg%d" % i, [128, TT], F32) for i in range(2)]
        cs = [sb("cs%d" % i, [128, 4, 64], F32) for i in range(2)]
        QTs = [sb("QTs%d" % i, [128, 4, TT], BF16) for i in range(2)]
        msT = sb("msT", [128, 4, TT], BF16)
        stats = sb("stats", [128, 64], F32)
        arena_f = sb("arena_f", [128, 7296], F32)
        arena_b = sb("arena_b", [128, 3072], BF16)
        ps = [es.enter_context(nc.psum_tensor("ps%d" % i, [128, 2, 512], F32)) for i in range(4)]

        def bank(i):
            return ps[i // 2][:, i % 2, :]

        def bank_bf(i):
            return ps[i // 2][:, i % 2, :].bitcast(BF16)

        def PSK(i):
            return ("ps", i)

        o = [0]

        def carve(n):
            a = arena_f[:, o[0]:o[0] + n]
            o[0] += n
            return a
        qsb = carve(512); sq = carve(512); ksb = carve(128); zsq = carve(512); zt = carve(512)
        u2 = carve(2048); v2 = carve(512); ftmp = carve(512); sgu = carve(512)
        rt1 = carve(256); rt2 = carve(256); rt3 = carve(256); rt4 = carve(256)
        assert o[0] <= 7296, o[0]
        vvh = arena_b[:, 0:512]; ms = arena_b[:, 512:1024]; qrot = arena_b[:, 1024:1536]; krot = arena_b[:, 1536:1664]
        Osb = arena_f[:, 0:1024].rearrange("p (h t) -> p h t", h=2)
        attn_tm = arena_f[:, 1024:3072].rearrange("p (s c) -> p s c", s=4)
        Pbuf = [arena_b[:, i * 1024:(i + 1) * 1024] for i in range(3)]

        st_ssq = stats[:, 0:4]; st_ms = stats[:, 4:8]; st_rstd = stats[:, 8:12]
        st_q = stats[:, 12:22]; st_qm = stats[:, 22:32]; st_qr = stats[:, 32:42]
        st_v = stats[:, 42:43]; st_vm = stats[:, 43:44]; st_vr = stats[:, 44:45]
        st_s = stats[:, 45:46]; st_sm = stats[:, 46:47]; st_sr = stats[:, 47:48]
        st_rc = stats[:, 48:56]

        stats2 = sb("stats2", [128, 16], F32)
        st_sets = [(st_ssq, st_ms, st_rstd), (stats2[:, 0:4], stats2[:, 4:8], stats2[:, 8:12])]

        def xk(b, sub):
            return [("xt", b, sub, q) for q in range(4)]

        def flat2048(ap):
            n = ap.shape[0] * ap.shape[1] // 2048
            return ap.rearrange("a b -> (a b)").rearrange("(n c) -> n c", c=2048), n

        def cast_weight(nm):
            src, _ = flat2048(wf[nm]); dst, _ = flat2048(wb[nm])
            P.add("gpsimd", lambda e: e.dma_start(out=dst, in_=src), writes=[("wb", nm)], dma=("cast", nm))

        P.add("gpsimd", lambda e: e.memset(ident_b[:], 0.0), writes=[("ident_b",)])
        P.add("gpsimd", lambda e: e.affine_select(out=ident_b[:], in_=ident_b[:], pattern=[[-1, 128]], compare_op=ALU.not_equal,
                                                   fill=1.0, base=0, channel_multiplier=1), reads=[("ident_b",)], writes=[("ident_b",)])
        P.add("gpsimd", lambda e: e.memset(ident_f[:], 0.0), writes=[("ident_f",)])
        P.add("gpsimd", lambda e: e.affine_select(out=ident_f[:], in_=ident_f[:], pattern=[[-1, 128]], compare_op=ALU.not_equal,
                                                   fill=1.0, base=0, channel_multiplier=1), reads=[("ident_f",)], writes=[("ident_f",)])
        P.add("gpsimd", lambda e: e.memset(mhalf[:], -0.5), writes=[("mhalf",)])
        P.add("gpsimd", lambda e: e.memset(VA[:].rearrange("p a b c -> p (a b) c")[:, :, 64:65], 1.0), writes=[("VA1",)])
        for nm in ("w1_gate", "w1_up", "w1_down", "w_in", "w_out"):
            cast_weight(nm)

        def load_gain(dst, src, key):
            P.add("sync", lambda e: e.dma_start(out=dst[:], in_=src.partition_broadcast(128)), writes=[key], dma=("g",) + key)
        load_gain(g1, g_ffn1_d, ("g1",)); load_gain(g2, g_mix_d, ("g2",)); load_gain(g3, g_sgu_d, ("g3",))
        load_gain(g4, g_so_d, ("g4",)); load_gain(gq, g_q_d, ("gq",)); load_gain(gk, g_k_d, ("gk",))

        def ld_bs(e):
            with nc.allow_non_contiguous_dma(reason="tiny b_s transpose"):
                return e.dma_start(out=bsT[:], in_=bs_d.rearrange("g p -> p g"))
        P.add("sync", ld_bs, writes=[("bsT",)], dma=("bsT",))
        ws_f = arena_f[:, 0:1024].rearrange("p (g q) -> p g q", g=8)
        ws_b = arena_b[:, 0:1024].rearrange("p (g q) -> p g q", g=8)
        P.add("sync", lambda e: e.dma_start(out=ws_f, in_=ws_d.rearrange("g p q -> p g q")), writes=[("ws_f",)], dma=("ws_f",))
        P.add("vector", lambda e: e.tensor_copy(out=ws_b, in_=ws_f), reads=[("ws_f",)], writes=[("ws_b",)])

        def ws_tr(e):
            for g in range(8):
                i = e.transpose(out=bank_bf(6)[:, g * 128:(g + 1) * 128], in_=ws_b[:, g, :], identity=ident_b[:])
            return i
        P.add("tensor", ws_tr, reads=[("ws_b",), ("ident_b",)], writes=[PSK(6)])
        P.add("vector", lambda e: e.tensor_copy(out=WsT[:].rearrange("p g q -> p (g q)"), in_=bank_bf(6)), reads=[PSK(6)], writes=[("WsT",)])
        P.fence()

        stage_ctr = {}

        def stage_load(kind, idx, names, slots=(0, 1, 2)):
            n_ = stage_ctr.get(slots, 0)
            stage_ctr[slots] = n_ + 1
            slot = slots[n_ % len(slots)]
            r = ring[slot]
            key = ("ring", slot)
            key2 = ("ring2", slot)
            if kind == "U":
                wg = wb[names[0]].rearrange("(kt p) n -> p kt n", p=128)[:, :, idx * 256:(idx + 1) * 256]
                wu = wb[names[1]].rearrange("(kt p) n -> p kt n", p=128)[:, :, idx * 256:(idx + 1) * 256]
                P.add("sync", lambda e: e.dma_start(out=r[:, 0:2048].rearrange("p (k n) -> p k n", k=8), in_=wg),
                      reads=[("wb", names[0])], writes=[key], dma=("ringd", slot))
                P.add("sync", lambda e: e.dma_start(out=r[:, 2048:4096].rearrange("p (k n) -> p k n", k=8), in_=wu),
                      reads=[("wb", names[1])], writes=[key2], dma=("ringd2", slot))
            elif kind == "D":
                wd = wb[names[0]].rearrange("(ft p) n -> p ft n", p=128)[:, :, idx * 256:(idx + 1) * 256]
                P.add("sync", lambda e: e.dma_start(out=r[:, 0:5632].rearrange("p (f n) -> p f n", f=NFT), in_=wd),
                      reads=[("wb", names[0])], writes=[key, key2], dma=("ringd", slot))
            elif kind == "IN":
                c0, w = idx
                wi = wb["w_in"].rearrange("(kt p) n -> p kt n", p=128)[:, :, c0:c0 + w]
                P.add("sync", lambda e: e.dma_start(out=r[:, 0:8 * w].rearrange("p (k n) -> p k n", k=8), in_=wi),
                      reads=[("wb", "w_in")], writes=[key, key2], dma=("ringd", slot))
            elif kind == "OUT":
                wo = wb["w_out"].rearrange("(kt p) n -> p kt n", p=128)[:, idx * 4:(idx + 1) * 4, :]
                P.add("sync", lambda e: e.dma_start(out=r[:, 0:4096].rearrange("p (k n) -> p k n", k=4), in_=wo),
                      reads=[("wb", "w_out")], writes=[key, key2], dma=("ringd", slot))
            return slot

        def emit_rstd(ssq_ap, tmp_ap, rstd_ap, n, scale, eps, keys):
            ks, km, kr = keys
            P.add("vector", lambda e: e.tensor_scalar(out=tmp_ap, in0=ssq_ap, scalar1=scale, scalar2=eps, op0=ALU.mult, op1=ALU.add),
                  reads=[ks], writes=[km])
            P.add("scalar", lambda e: e.activation(out=tmp_ap, in_=tmp_ap, func=AF.Sqrt), reads=[km], writes=[km])
            P.add("vector", lambda e: e.reciprocal(out=rstd_ap, in_=tmp_ap), reads=[km], writes=[kr])

        def drain(g):
            for _ in g:
                pass

        def emit_norm_T(b, gain, gkey, dstT, dkey, tb=6, st=0, xp=None):
            ssq_, ms_, rstd_ = st_sets[st]
            if xp is None:
                xp = ((xn[0][:], [("xn", 0)]), (xn[1][:], [("xn", 1)]))
            for sub in range(4):
                P.add("scalar", lambda e, sub=sub: e.activation(out=xp[sub % 2][0], in_=xt[b][:, sub, :], func=AF.Square,
                                                                 accum_out=ssq_[:, sub:sub + 1]),
                      reads=xk(b, sub), writes=xp[sub % 2][1] + [("st_ssq", st, sub)])
            yield 0
            P.add("vector", lambda e: e.tensor_scalar(out=ms_, in0=ssq_, scalar1=1.0 / D, scalar2=EPS, op0=ALU.mult, op1=ALU.add),
                  reads=[("st_ssq", st, s_) for s_ in range(4)], writes=[("st_ms", st)])
            P.add("scalar", lambda e: e.activation(out=ms_, in_=ms_, func=AF.Sqrt), reads=[("st_ms", st)], writes=[("st_ms", st)])
            P.add("vector", lambda e: e.reciprocal(out=rstd_, in_=ms_), reads=[("st_ms", st)], writes=[("st_rstd", st)])

            def normalize(sub):
                xb, xkeys = xp[sub % 2]
                P.add("vector", lambda e: e.scalar_tensor_tensor(out=xb, in0=xt[b][:, sub, :], scalar=rstd_[:, sub:sub + 1],
                                                                 in1=gain[:], op0=ALU.mult, op1=ALU.mult),
                      reads=xk(b, sub) + [("st_rstd", st), gkey], writes=xkeys)

            def transp(sub):
                xb, xkeys = xp[sub % 2]

                def tr(e):
                    for k in range(8):
                        i = e.transpose(out=bank_bf(tb)[:, k * 128:(k + 1) * 128], in_=xb[:, k * 128:(k + 1) * 128], identity=ident_b[:])
                    return i
                P.add("tensor", tr, reads=xkeys + [("ident_b",)], writes=[PSK(tb)])
                P.add("scalar", lambda e: e.activation(out=dstT[:, :, sub * 128:(sub + 1) * 128],
                                                       in_=bank_bf(tb).rearrange("p (k t) -> p k t", k=8), func=AF.Copy),
                      reads=[PSK(tb)], writes=[(dkey, sub)])
            normalize(0)
            yield 0
            for sub in range(4):
                transp(sub)
                if sub + 1 < 4:
                    normalize(sub + 1)
                yield 0

        def emit_ffn(b, names, gbs=(0, 1), ubs=(2, 3), ybs=(4, 5), use_tanh=False, slots=(0, 1, 2), split=1, dsplit=1):
            for j in range(11):
                slot = stage_load("U", j, names[0:2], slots)
                r = ring[slot]
                for t in range(2):
                    f = 2 * j + t
                    gb, ub = gbs[f % len(gbs)], ubs[f % len(ubs)]

                    def mmg(e, r=r, t=t, gb=gb):
                        for k in range(8):
                            i = e.matmul(bank(gb), lhsT=r[:, k * 256 + t * 128:k * 256 + (t + 1) * 128], rhs=xnT[:, k, :], start=(k == 0), stop=(k == 7))
                        return i

                    def mmu(e, r=r, t=t, ub=ub):
                        for k in range(8):
                            i = e.matmul(bank(ub), lhsT=r[:, 2048 + k * 256 + t * 128:2048 + k * 256 + (t + 1) * 128], rhs=xnT[:, k, :],
                                         start=(k == 0), stop=(k == 7))
                        return i
                    if use_tanh:
                        kc = 8 // split
                        for c_ in range(split):
                            def mmg_c(e, r=r, t=t, gb=gb, c_=c_):
                                for k in range(c_ * kc, (c_ + 1) * kc):
                                    i = e.matmul(bank(gb), lhsT=r[:, k * 256 + t * 128:k * 256 + (t + 1) * 128], rhs=xnT[:, k, :], start=(k == 0), stop=(k == 7))
                                return i
                            P.add("tensor", mmg_c, reads=[("ring", slot)] + [("xnT", s_) for s_ in range(4)], writes=[PSK(gb)])
                            if c_ == split - 1:
                                P.add("scalar", lambda e, f=f, gb=gb: e.activation(out=sg[f % 2][:], in_=bank(gb), func=AF.Tanh, scale=0.5),
                                      reads=[PSK(gb)], writes=[("sg", f % 2)])
                                P.add("vector", lambda e, f=f, gb=gb: e.scalar_tensor_tensor(out=sg[f % 2][:], in0=sg[f % 2][:], scalar=1.0, in1=bank(gb),
                                                                                          op0=ALU.add, op1=ALU.mult),
                                      reads=[("sg", f % 2), PSK(gb)], writes=[("sg", f % 2)])
                            yield
                        for c_ in range(split):
                            def mmu_c(e, r=r, t=t, ub=ub, c_=c_):
                                for k in range(c_ * kc, (c_ + 1) * kc):
                                    i = e.matmul(bank(ub), lhsT=r[:, 2048 + k * 256 + t * 128:2048 + k * 256 + (t + 1) * 128], rhs=xnT[:, k, :],
                                                 start=(k == 0), stop=(k == 7))
                                return i
                            P.add("tensor", mmu_c, reads=[("ring2", slot)] + [("xnT", s_) for s_ in range(4)], writes=[PSK(ub)])
                            if c_ == split - 1:
                                P.add("vector", lambda e, f=f, ub=ub: e.scalar_tensor_tensor(out=hT[:, f, :], in0=sg[f % 2][:], scalar=0.5, in1=bank(ub),
                                                                                          op0=ALU.mult, op1=ALU.mult),
                                      reads=[("sg", f % 2), PSK(ub)], writes=[("hT", f)])
                            yield
                    else:
                        P.add("tensor", mmg, reads=[("ring", slot)] + [("xnT", s_) for s_ in range(4)], writes=[PSK(gb)])
                        P.add("scalar", lambda e, f=f, gb=gb: e.activation(out=sg[f % 2][:], in_=bank(gb), func=AF.Silu),
                              reads=[PSK(gb)], writes=[("sg", f % 2)])
                        P.add("tensor", mmu, reads=[("ring2", slot)] + [("xnT", s_) for s_ in range(4)], writes=[PSK(ub)])
                        P.add("vector", lambda e, f=f, ub=ub: e.tensor_tensor(out=hT[:, f, :], in0=sg[f % 2][:], in1=bank(ub), op=ALU.mult),
                              reads=[("sg", f % 2), PSK(ub)], writes=[("hT", f)])
                        yield
            n = 0
            for qd in range(4):
                slot = stage_load("D", qd, names[2:3], slots)
                r = ring[slot]
                for sub in range(4):
                    yb = ybs[n % len(ybs)]
                    n += 1

                    bounds = [(NFT * c_) // dsplit for c_ in range(dsplit + 1)]
                    for c_ in range(dsplit):
                        def mm(e, r=r, sub=sub, yb=yb, lo=bounds[c_], hi=bounds[c_ + 1]):
                            for f in range(lo, hi):
                                i = e.matmul(bank(yb)[:, 0:256], lhsT=hT[:, f, sub * 128:(sub + 1) * 128], rhs=r[:, f * 256:(f + 1) * 256],
                                             start=(f == 0), stop=(f == NFT - 1))
                            return i
                        P.add("tensor", mm, reads=[("ring", slot), ("ring2", slot)] + [("hT", f) for f in range(NFT)], writes=[PSK(yb)])
                        if c_ == dsplit - 1:
                            P.add("vector", lambda e, sub=sub, qd=qd, yb=yb: e.scalar_tensor_tensor(
                                out=xt[b][:, sub, qd * 256:(qd + 1) * 256], in0=bank(yb)[:, 0:256], scalar=0.5,
                                in1=xt[b][:, sub, qd * 256:(qd + 1) * 256], op0=ALU.mult, op1=ALU.add),
                                reads=[PSK(yb), ("xt", b, sub, qd)], writes=[("xt", b, sub, qd)])
                        yield

        def emit_outproj(b, src_T, skey, half, ybs=(4, 5), slots=(0, 1, 2)):
            slot = stage_load("OUT", half, None, slots)
            r = ring[slot]
            n = 0
            for sub in range(4):
                for hf in range(2):
                    yb = ybs[n % len(ybs)]
                    n += 1

                    def mm(e, r=r, sub=sub, hf=hf, yb=yb):
                        for k in range(4):
                            i = e.matmul(bank(yb), lhsT=src_T[:, k, sub * 128:(sub + 1) * 128], rhs=r[:, k * 1024 + hf * 512:k * 1024 + (hf + 1) * 512],
                                         start=(k == 0), stop=(k == 3))
                        return i
                    P.add("tensor", mm, reads=[("ring", slot), ("ring2", slot), (skey, sub)], writes=[PSK(yb)])
                    P.add("vector", lambda e, sub=sub, hf=hf, yb=yb: e.tensor_tensor(
                        out=xt[b][:, sub, hf * 512:(hf + 1) * 512], in0=bank(yb), in1=xt[b][:, sub, hf * 512:(hf + 1) * 512], op=ALU.add),
                        reads=[PSK(yb), ("xt", b, sub, 2 * hf), ("xt", b, sub, 2 * hf + 1)],
                        writes=[("xt", b, sub, 2 * hf), ("xt", b, sub, 2 * hf + 1)])
                    yield

        def emit_qk(src_bank, nh, sbuf_f, gain, gkey, csb, sub, rot_out_view, rkey, tagk):
            w = nh * 64
            srcp = bank(src_bank)[:, 0:w] if isinstance(src_bank, int) else src_bank
            P.add("scalar", lambda e: e.activation(out=sq[:, 0:w], in_=srcp, func=AF.Square), reads=[tagk], writes=[("sq",)])
            P.add("scalar", lambda e: e.activation(out=sbuf_f[:, 0:w], in_=srcp, func=AF.Copy), reads=[tagk], writes=[("qk_f",)])
            P.add("vector", lambda e: e.tensor_reduce(out=st_q[:, 0:nh], in_=sq[:, 0:w].rearrange("p (h d) -> p h d", h=nh), op=ALU.add, axis=AX.X),
                  reads=[("sq",)], writes=[("st_q",)])
            emit_rstd(st_q[:, 0:nh], st_qm[:, 0:nh], st_qr[:, 0:nh], nh, 1.0 / 64, EPS, (("st_q",), ("st_qm",), ("st_qr",)))
            v3 = sbuf_f[:, 0:w].rearrange("p (h d) -> p h d", h=nh)
            P.add("vector", lambda e: e.tensor_tensor(out=v3, in0=v3, in1=st_qr[:, 0:nh].unsqueeze(2).to_broadcast([128, nh, 64]), op=ALU.mult),
                  reads=[("qk_f",), ("st_qr",)], writes=[("qk_f",)])
            P.add("vector", lambda e: e.tensor_tensor(out=v3, in0=v3, in1=gain[:].unsqueeze(1).to_broadcast([128, nh, 64]), op=ALU.mult),
                  reads=[("qk_f",), gkey], writes=[("qk_f",)])
            v4 = sbuf_f[:, 0:w].rearrange("p (h i t) -> p h i t", h=nh, t=2)
            x1 = v4[:, :, :, 0]
            x2 = v4[:, :, :, 1]
            cosb = csb[:, sub, 0:32].unsqueeze(1).to_broadcast([128, nh, 32])
            sinb = csb[:, sub, 32:64].unsqueeze(1).to_broadcast([128, nh, 32])
            hw = nh * 32
            t1 = rt1[:, 0:hw].rearrange("p (h i) -> p h i", h=nh); t2 = rt2[:, 0:hw].rearrange("p (h i) -> p h i", h=nh)
            t3 = rt3[:, 0:hw].rearrange("p (h i) -> p h i", h=nh); t4 = rt4[:, 0:hw].rearrange("p (h i) -> p h i", h=nh)
            ck = ("cs", id(csb))
            P.add("vector", lambda e: e.tensor_tensor(out=t1, in0=x1, in1=cosb, op=ALU.mult), reads=[("qk_f",), ck], writes=[("rt1",)])
            P.add("vector", lambda e: e.tensor_tensor(out=t2, in0=x2, in1=sinb, op=ALU.mult), reads=[("qk_f",), ck], writes=[("rt2",)])
            P.add("vector", lambda e: e.tensor_tensor(out=t3, in0=x1, in1=sinb, op=ALU.mult), reads=[("qk_f",), ck], writes=[("rt3",)])
            P.add("vector", lambda e: e.tensor_tensor(out=t4, in0=x2, in1=cosb, op=ALU.mult), reads=[("qk_f",), ck], writes=[("rt4",)])
            P.add("vector", lambda e: e.tensor_tensor(out=rot_out_view(0), in0=rot_in_view(t1, nh), in1=rot_in_view(t2, nh), op=ALU.subtract),
                  reads=[("rt1",), ("rt2",)], writes=[rkey])
            P.add("vector", lambda e: e.tensor_tensor(out=rot_out_view(1), in0=rot_in_view(t3, nh), in1=rot_in_view(t4, nh), op=ALU.add),
                  reads=[("rt3",), ("rt4",)], writes=[rkey])

        def rot_in_view(t, nh):
            if nh == 8:
                return t.rearrange("p (a b) i -> p a b i", a=2)
            return t

        def q_out_view(tt):
            return qrot.rearrange("p (b a i t) -> p a b i t", b=4, a=2, t=2)[:, :, :, :, tt]

        def k_out_view(tt):
            return krot.rearrange("p (h i t) -> p h i t", h=2, t=2)[:, :, :, tt]

        def load_tile_a(i):
            b = i % 2
            P.add("gpsimd", lambda e: e.dma_start(out=xt[b][:], in_=tiled(x_d)[i]), writes=[k_ for s_ in range(4) for k_ in xk(b, s_)], dma=("xld", b))
            P.add("gpsimd", lambda e: e.dma_start(out=cs[b][:], in_=tiled(rope_d)[i]), writes=[("cs", id(cs[b]))], dma=("csld", b))

        def main_gen(i):
            b = i % 2
            yield from emit_norm_T(b, g1, ("g1",), xnT, "xnT", tb=4, st=0)
            yield from emit_ffn(b, ("w1_gate", "w1_up", "w1_down"), gbs=(0, 1), ubs=(2, 3), ybs=(4,), slots=(0, 1))

        def bg_gen(i):
            b = i % 2
            csb = cs[b]
            BS = (2,)
            xpb = ((arena_b[:, 0:1024], [("vvh",), ("ms",)]), (arena_b[:, 2048:3072], [("xnb1",)]))
            junk_b = arena_b[:, 2048:2560]
            yield from emit_norm_T(b, g2, ("g2",), hnT, "hnT", tb=7, st=1, xp=xpb)

            def proj(slot, sub, w, pb):
                r = ring[slot]

                def mm(e):
                    for k in range(8):
                        i_ = e.matmul(bank(pb)[:, 0:w], lhsT=hnT[:, k, sub * 128:(sub + 1) * 128], rhs=r[:, k * w:(k + 1) * w], start=(k == 0), stop=(k == 7))
                    return i_
                P.add("tensor", mm, reads=[("ring", slot), ("ring2", slot), ("hnT", sub)], writes=[PSK(pb)])

            def q_finish(sub):
                def trq(e):
                    for pr in range(4):
                        i_ = e.transpose(out=bank_bf(7)[:, pr * 128:(pr + 1) * 128], in_=qrot[:, pr * 128:(pr + 1) * 128], identity=ident_b[:])
                    return i_
                P.add("tensor", trq, reads=[("qrot",), ("ident_b",)], writes=[PSK(7)])
                P.add("vector", lambda e: e.tensor_copy(out=QTs[0][:, :, sub * 128:(sub + 1) * 128],
                                                        in_=bank_bf(7)[:, 0:512].rearrange("p (a t) -> p a t", a=4)),
                      reads=[PSK(7)], writes=[("QTs", 0, sub)])
            slot = stage_load("IN", (0, 512), None, BS)
            for sub in range(4):
                pb = 5 + sub % 2
                proj(slot, sub, 512, pb)
                if sub > 0:
                    q_finish(sub - 1)
                yield 0
                emit_qk(pb, 8, qsb, gq, ("gq",), csb, sub, q_out_view, ("qrot",), PSK(pb))
                yield BG_DELAY
            q_finish(3)
            P.add("gpsimd", lambda e: e.dma_start(out=qts_d.rearrange("a p t -> p a t")[:, :, i * TT:(i + 1) * TT], in_=QTs[0][:]),
                  reads=[("QTs", 0, s_) for s_ in range(4)], writes=[("qts", i)], dma=("qst",))
            def k_finish(sub):
                P.add("tensor", lambda e: e.transpose(out=bank_bf(7)[:, 0:128], in_=krot[:, 0:128], identity=ident_b[:]),
                      reads=[("krot",), ("ident_b",)], writes=[PSK(7)])
                tg = i * 4 + sub
                P.add("vector", lambda e: e.tensor_copy(out=KT[:, tg * 128:(tg + 1) * 128], in_=bank_bf(7)[:, 0:128]),
                      reads=[PSK(7)], writes=[("KT", tg)])
            slot = stage_load("IN", (512, 256), None, BS)
            for sub in range(4):
                pb = 5 + sub % 2
                proj(slot, sub, 256, pb)
                if sub > 0:
                    k_finish(sub - 1)
                yield 0
                tg = i * 4 + sub
                P.add("scalar", lambda e, tg=tg, pb=pb: e.activation(out=VA[:, tg, :, 0:64], in_=bank(pb)[:, 128:256].rearrange("p (h d) -> p h d", h=2),
                                                                     func=AF.Copy),
                      reads=[PSK(pb), ("VA1",)], writes=[("VA", tg)])
                emit_qk(bank(pb)[:, 0:128], 2, ksb, gk, ("gk",), csb, sub, k_out_view, ("krot",), PSK(pb))
                yield BG_DELAY
            k_finish(3)

            def emit_gelu2(pb, dst, dkey):
                src = bank(pb)
                P.add("scalar", lambda e: e.activation(out=zsq, in_=src, func=AF.Square), reads=[PSK(pb)], writes=[("zsq",)])
                P.add("vector", lambda e: e.tensor_scalar(out=zsq, in0=zsq, scalar1=0.044715, scalar2=1.0, op0=ALU.mult, op1=ALU.add),
                      reads=[("zsq",)], writes=[("zsq",)])
                P.add("vector", lambda e: e.tensor_tensor(out=zt, in0=zsq, in1=src, op=ALU.mult), reads=[("zsq",), PSK(pb)], writes=[("zt",)])
                P.add("scalar", lambda e: e.activation(out=zt, in_=zt, func=AF.Tanh, scale=GELU_C), reads=[("zt",)], writes=[("zt",)])
                P.add("vector", lambda e: e.scalar_tensor_tensor(out=dst, in0=zt, scalar=1.0, in1=src, op0=ALU.add, op1=ALU.mult),
                      reads=[("zt",), PSK(pb)], writes=[dkey])

            slot = stage_load("IN", (768, 512), None, BS)
            for sub in range(4):
                pb = 5 + sub % 2
                proj(slot, sub, 512, pb)
                yield 0
                emit_gelu2(pb, u2[:, sub * 512:(sub + 1) * 512], ("u2", sub))
                yield 0
            def zv_spat(sub):
                def spat(e):
                    for g in range(8):
                        i_ = e.matmul(bank(7)[:, g * 64:(g + 1) * 64], lhsT=WsT[:, g, :], rhs=vvh[:, g * 64:(g + 1) * 64], start=True, stop=True)
                    return i_
                P.add("tensor", spat, reads=[("vvh",), ("WsT",)], writes=[PSK(7)])
                P.add("vector", lambda e: e.tensor_tensor(out=ftmp, in0=bank(7), in1=g3[:], op=ALU.mult), reads=[PSK(7), ("g3",)], writes=[("ftmp",)])
                f3 = ftmp.rearrange("p (g d) -> p g d", g=8)
                P.add("vector", lambda e: e.tensor_tensor(out=f3, in0=f3, in1=bsT[:].unsqueeze(2).to_broadcast([128, 8, 64]), op=ALU.add),
                      reads=[("ftmp",), ("bsT",)], writes=[("ftmp",)])
                P.add("vector", lambda e: e.scalar_tensor_tensor(out=sgu, in0=ftmp, scalar=0.5, in1=u2[:, sub * 512:(sub + 1) * 512],
                                                                 op0=ALU.mult, op1=ALU.mult),
                      reads=[("ftmp",), ("u2", sub)], writes=[("sgu",)])
                P.add("scalar", lambda e: e.activation(out=junk_b, in_=sgu, func=AF.Square, accum_out=st_s), reads=[("sgu",)], writes=[("xnb1",), ("st_s",)])
                emit_rstd(st_s, st_sm, st_sr, 1, 1.0 / 512, EPS, (("st_s",), ("st_sm",), ("st_sr",)))
                P.add("vector", lambda e: e.scalar_tensor_tensor(out=ms, in0=sgu, scalar=st_sr, in1=g4[:], op0=ALU.mult, op1=ALU.mult),
                      reads=[("sgu",), ("st_sr",), ("g4",)], writes=[("ms",)])

            def zv_finish(sub):
                def trm(e):
                    for k in range(4):
                        i_ = e.transpose(out=bank_bf(7)[:, k * 128:(k + 1) * 128], in_=ms[:, k * 128:(k + 1) * 128], identity=ident_b[:])
                    return i_
                P.add("tensor", trm, reads=[("ms",), ("ident_b",)], writes=[PSK(7)])
                P.add("scalar", lambda e: e.activation(out=msT[:, :, sub * 128:(sub + 1) * 128],
                                                       in_=bank_bf(7)[:, 0:512].rearrange("p (a t) -> p a t", a=4), func=AF.Copy),
                      reads=[PSK(7)], writes=[("msT", sub)])
            slot = stage_load("IN", (1280, 512), None, BS)
            for sub in range(4):
                pb = 5 + sub % 2
                proj(slot, sub, 512, pb)
                yield 0
                emit_gelu2(pb, v2, ("v2",))
                P.add("scalar", lambda e: e.activation(out=junk_b, in_=v2, func=AF.Square, accum_out=st_v), reads=[("v2",)], writes=[("xnb1",), ("st_v",)])
                emit_rstd(st_v, st_vm, st_vr, 1, 1.0 / 512, 4 * EPS, (("st_v",), ("st_vm",), ("st_vr",)))
                P.add("vector", lambda e: e.tensor_scalar(out=vvh, in0=v2, scalar1=st_vr, scalar2=None, op0=ALU.mult), reads=[("v2",), ("st_vr",)], writes=[("vvh",)])
                yield BG_DELAY
                zv_spat(sub)
                yield BG_DELAY
                zv_finish(sub)
                yield 0
            yield from emit_outproj(b, msT, "msT", 1, ybs=(5, 6), slots=BS)
            P.add("gpsimd", lambda e: e.dma_start(out=tiled(x1s_d)[i], in_=xt[b][:]),
                  reads=[k_ for s_ in range(4) for k_ in xk(b, s_)], writes=[("x1s", i)], dma=("xst", b))
            yield 0

        if n_tiles_a > 0:
            load_tile_a(0)
            if n_tiles_a > 1:
                load_tile_a(1)
            drain(main_gen(0))
        for i in range(n_tiles_a):
            bg = bg_gen(i)
            main = main_gen(i + 1) if i + 1 < n_tiles_a else None
            bg_done = False
            wait = 0
            if main is not None:
                for _ in main:
                    if bg_done:
                        continue
                    if wait > 0:
                        wait -= 1
                        continue
                    try:
                        wait = next(bg) or 0
                    except StopIteration:
                        bg_done = True
                        if i + 2 < n_tiles_a:
                            load_tile_a(i + 2)
            if not bg_done:
                drain(bg)
                if i + 2 < n_tiles_a:
                    load_tile_a(i + 2)
            if i == min(2, n_tiles_a - 1):
                for nm in ("w2_gate", "w2_up", "w2_down"):
                    cast_weight(nm)

        if debug:
            P.add("gpsimd", lambda e: e.dma_start(out=kt_dbg, in_=KT[:]), reads=[("KT", t_) for t_ in range(NKT)], dma=("dbg1",))
            P.add("gpsimd", lambda e: e.dma_start(out=va_dbg, in_=VA[:].rearrange("p a b c -> p (a b c)")),
                  reads=[("VA", t_) for t_ in range(NKT)] + [("VA1",)], dma=("dbg2",))
        P.fence()

        load_gain(g1, g_ffn2_d, ("g1",)); load_gain(g2, g_final_d, ("g2",)); load_gain(g3, g_ao_d, ("g3",))

        def load_x_b(c):
            b = c % 2
            P.add("gpsimd", lambda e: e.dma_start(out=xt[b][:], in_=tiled(x1s_d)[c]), reads=[("x1s", c)],
                  writes=[k_ for s_ in range(4) for k_ in xk(b, s_)], dma=("xld", b))

        def load_q_b(c):
            b = c % 2
            P.add("gpsimd", lambda e: e.dma_start(out=QTs[b][:], in_=qts_d.rearrange("a p t -> p a t")[:, :, c * TT:(c + 1) * TT]),
                  reads=[("qts", c)], writes=[("QTs", b, s_) for s_ in range(4)], dma=("qld", b))

        def attention_gen(c):
            b = c % 2
            Q = QTs[b]

            def emit_qk_mm(n):
                pr, kt = divmod(n, NKT)
                sp = n % 2

                def qk(e):
                    e.matmul(bank(2 * sp), lhsT=KT[0:64, kt * 128:(kt + 1) * 128], rhs=Q[0:64, pr, :], start=True, stop=True)
                    return e.matmul(bank(2 * sp + 1), lhsT=KT[64:128, kt * 128:(kt + 1) * 128], rhs=Q[64:128, pr, :], start=True, stop=True)
                P.add("tensor", qk, reads=[("KT", kt)] + [("QTs", b, s_) for s_ in range(4)], writes=[PSK(2 * sp), PSK(2 * sp + 1)])

            def emit_o_post(pr):
                def tro(e):
                    for h in range(2):
                        for sub in range(4):
                            i_ = e.transpose(out=bank(4 + h)[:, sub * 66:sub * 66 + 65], in_=Osb[0:65, h, sub * 128:(sub + 1) * 128], identity=ident_f[0:65, 0:65])
                    return i_
                P.add("tensor", tro, reads=[("Osb",), ("ident_f",)], writes=[PSK(4), PSK(5)])
                ov = ps[2][:, :, 0:264].rearrange("p h (s c) -> p h s c", c=66)
                P.add("vector", lambda e: e.reciprocal(out=st_rc.rearrange("p (h s) -> p h s", h=2), in_=ov[:, :, :, 64]),
                      reads=[PSK(4), PSK(5)], writes=[("st_rc",)])
                for h in range(2):
                    hd = pr + 4 * h
                    P.add("vector", lambda e, h=h, hd=hd: e.tensor_tensor(
                        out=attn_tm[:, :, hd * 64:(hd + 1) * 64], in0=ov[:, h, :, 0:64],
                        in1=st_rc[:, h * 4:(h + 1) * 4].unsqueeze(2).to_broadcast([128, 4, 64]), op=ALU.mult),
                        reads=[PSK(4 + h), ("st_rc",)], writes=[("attn_tm", hd)])

            NIT = 4 * NKT
            emit_qk_mm(0)
            for n in range(NIT):
                pr, kt = divmod(n, NKT)
                sp = n % 2
                pbuf = Pbuf[n % 3]
                if n + 1 < NIT:
                    emit_qk_mm(n + 1)
                P.add("scalar", lambda e, sp=sp, pbuf=pbuf: e.activation(out=pbuf, in_=ps[sp][:].rearrange("p a t -> p (a t)"), func=AF.Exp, scale=0.125),
                      reads=[PSK(2 * sp), PSK(2 * sp + 1)], writes=[("P", n % 3)])

                def pv(e, kt=kt, pbuf=pbuf):
                    e.matmul(bank(4)[0:65, :], lhsT=VA[:, kt, 0, :], rhs=pbuf[:, 0:512], start=(kt == 0), stop=(kt == NKT - 1))
                    return e.matmul(bank(5)[0:65, :], lhsT=VA[:, kt, 1, :], rhs=pbuf[:, 512:1024], start=(kt == 0), stop=(kt == NKT - 1))
                P.add("tensor", pv, reads=[("P", n % 3), ("VA", kt), ("VA1",)], writes=[PSK(4), PSK(5)])
                if kt == NKT - 1:
                    P.add("vector", lambda e: e.tensor_copy(out=Osb[0:65, :, :], in_=ps[2][0:65, :, :]), reads=[PSK(4), PSK(5)], writes=[("Osb",)])
                    emit_o_post(pr)
                yield
            if debug:
                P.add("gpsimd", lambda e: e.dma_start(out=tiled(attn_dbg)[c], in_=attn_tm), reads=[("attn_tm", h_) for h_ in range(8)], dma=("dbg3",))
            yield

        def post_gen(c):
            b = c % 2
            for sub in range(4):
                P.add("scalar", lambda e, sub=sub: e.activation(out=xn[0][:, 0:512], in_=attn_tm[:, sub, :], func=AF.Square, accum_out=st_s),
                      reads=[("attn_tm", h_) for h_ in range(8)], writes=[("xn", 0), ("st_s",)])
                emit_rstd(st_s, st_sm, st_sr, 1, 1.0 / 512, EPS, (("st_s",), ("st_sm",), ("st_sr",)))
                P.add("vector", lambda e, sub=sub: e.scalar_tensor_tensor(out=xn[1][:, 0:512], in0=attn_tm[:, sub, :], scalar=st_sr, in1=g3[:], op0=ALU.mult, op1=ALU.mult),
                      reads=[("attn_tm", h_) for h_ in range(8)] + [("st_sr",), ("g3",)], writes=[("xn", 1)])

                def trm2(e):
                    for k in range(4):
                        i_ = e.transpose(out=bank_bf(6)[:, k * 128:(k + 1) * 128], in_=xn[1][:, k * 128:(k + 1) * 128], identity=ident_b[:])
                    return i_
                P.add("tensor", trm2, reads=[("xn", 1), ("ident_b",)], writes=[PSK(6)])
                P.add("scalar", lambda e, sub=sub: e.activation(out=msT[:, :, sub * 128:(sub + 1) * 128],
                                                                 in_=bank_bf(6)[:, 0:512].rearrange("p (a t) -> p a t", a=4), func=AF.Copy),
                      reads=[PSK(6)], writes=[("msT", sub)])
                yield
            yield from emit_outproj(b, msT, "msT", 0, ybs=(6, 7))
            if debug:
                P.add("gpsimd", lambda e: e.dma_start(out=tiled(x2_dbg)[c], in_=xt[b][:]), reads=[k_ for s_ in range(4) for k_ in xk(b, s_)], dma=("dbg4",))
            yield from emit_norm_T(b, g1, ("g1",), xnT, "xnT", tb=6)
            yield from emit_ffn(b, ("w2_gate", "w2_up", "w2_down"), gbs=(6,), ubs=(7,), ybs=(6, 7), use_tanh=True, split=4, dsplit=4)
            if debug:
                P.add("gpsimd", lambda e: e.dma_start(out=tiled(x3_dbg)[c], in_=xt[b][:]), reads=[k_ for s_ in range(4) for k_ in xk(b, s_)], dma=("dbg5",))
            for sub in range(4):
                P.add("scalar", lambda e, sub=sub: e.activation(out=xn[sub % 2][:], in_=xt[b][:, sub, :], func=AF.Square, accum_out=st_ssq[:, sub:sub + 1]),
                      reads=xk(b, sub), writes=[("xn", sub % 2), ("st_ssq", 0, sub)])
            yield
            P.add("vector", lambda e: e.tensor_scalar(out=st_ms, in0=st_ssq, scalar1=1.0 / D, scalar2=EPS, op0=ALU.mult, op1=ALU.add),
                  reads=[("st_ssq", 0, s_) for s_ in range(4)], writes=[("st_ms", 0)])
            P.add("scalar", lambda e: e.activation(out=st_ms, in_=st_ms, func=AF.Sqrt), reads=[("st_ms", 0)], writes=[("st_ms", 0)])
            P.add("vector", lambda e: e.reciprocal(out=st_rstd, in_=st_ms), reads=[("st_ms", 0)], writes=[("st_rstd", 0)])
            for sub in range(4):
                P.add("vector", lambda e, sub=sub: e.scalar_tensor_tensor(out=xt[b][:, sub, :], in0=xt[b][:, sub, :], scalar=st_rstd[:, sub:sub + 1],
                                                                          in1=g2[:], op0=ALU.mult, op1=ALU.mult),
                      reads=xk(b, sub) + [("st_rstd", 0), ("g2",)], writes=xk(b, sub))
                yield
            P.add("gpsimd", lambda e: e.dma_start(out=tiled(out_d)[c], in_=xt[b][:]),
                  reads=[k_ for s_ in range(4) for k_ in xk(b, s_)], writes=[("out", c)], dma=("xst", b))
            yield

        if n_tiles_b > 0:
            load_q_b(0)
            load_x_b(0)
        prev_post = None
        for c in range(n_tiles_b):
            if c + 1 < n_tiles_b:
                load_q_b(c + 1)
            att = attention_gen(c)
            if prev_post is None:
                drain(att)
            else:
                k = 0
                emitted = 0
                for _ in att:
                    k += 1
                    while emitted * 4 * NKT < k * POST_PIECES:
                        next(prev_post, None)
                        emitted += 1
                    assert k < NKT or emitted >= 4
                drain(prev_post)
            if c + 1 < n_tiles_b:
                load_x_b(c + 1)
            prev_post = post_gen(c)
        if prev_post is not None:
            drain(prev_post)

        P.fence()

        P.finalize(nc, es)
        with nc.Block() as block:
            @block.sync
            def _(e):
                P.replay("sync", e)

            @block.gpsimd
            def _(e):
                P.replay("gpsimd", e)

            @block.scalar
            def _(e):
                P.replay("scalar", e)

            @block.vector
            def _(e):
                P.replay("vector", e)

            @block.tensor
            def _(e):
                P.replay("tensor", e)
    return nc


def rope_table():
    rows = S // 64
    row_idx = np.repeat(np.arange(rows, dtype=np.float32), 64)
    col_idx = np.tile(np.arange(64, dtype=np.float32), rows)
    inv = (1.0 / (np.float32(10000.0) ** (np.arange(0, 32, 2, dtype=np.float32) / np.float32(32)))).astype(np.float32)
    ang = np.concatenate([row_idx[:, None] * inv, col_idx[:, None] * inv], axis=-1).astype(np.float32)
    return np.concatenate([np.cos(ang), np.sin(ang)], axis=-1).astype(np.float32)


def make_in_maps(inputs, n_cores=8):
    x = np.asarray(inputs["x"], dtype=np.float32)
    rope = rope_table()
    shared = {"rope": rope}
    for nm in ("g_ffn1", "g_mix", "g_ffn2", "g_final", "g_q", "g_k", "g_sgu", "g_attn_out", "g_sgu_out", "w_s", "b_s",
               "w1_gate", "w1_up", "w1_down", "w_in", "w_out", "w2_gate", "w2_up", "w2_down"):
        shared[nm] = np.ascontiguousarray(np.asarray(inputs[nm], dtype=np.float32)[0])
    maps = []
    for c in range(n_cores):
        m = dict(shared)
        m["x"] = np.ascontiguousarray(x[c])
        maps.append(m)
    return maps


def kernel(**inputs):
    nc = build_nc()
    in_maps = make_in_maps(inputs)
    res = run_bass_kernel_spmd(nc, in_maps, core_ids=list(range(8)))
    return np.stack([np.asarray(r["out"], dtype=np.float32) for r in res.results], axis=0)
```

```python
import numpy as np
from contextlib import ExitStack
import concourse.bass as bass
import concourse.mybir as mybir
from concourse.bass_utils import run_bass_kernel_spmd

F32 = mybir.dt.float32
BF16 = mybir.dt.bfloat16
AF = mybir.ActivationFunctionType
ALU = mybir.AluOpType
AX = mybir.AxisListType

S = 8192
D = 1024
DFF = 2816
NFT = DFF // 128
DIN = 1792
TT = 512
NT = S // TT
NKT = S // 128
EPS = 1e-6
RING = 3
SLOT = 5632
GELU_C = 0.7978845608028654
STRICT_SYNC = False
BG_DELAY = 0
POST_PIECES = 275


class _Op:
    __slots__ = ("eng", "fn", "dma", "deps", "need_inc", "tick", "waits")


class Prog:
    ENGS = ("tensor", "vector", "scalar", "gpsimd", "sync")

    def __init__(self):
        self.ops = []
        self.last_w = {}
        self.readers = {}
        self.dma_last = {}
        self.dma_keys = []
        self.last_eng = {}

    def add(self, eng, fn, reads=(), writes=(), dma=None, extra_deps=()):
        op = _Op()
        op.eng = eng
        op.fn = fn
        op.dma = dma
        op.need_inc = False
        op.tick = 0
        op.waits = []
        deps = {}
        for r in reads:
            w = self.last_w.get(r)
            if w is not None:
                deps[w] = True
        for wr in writes:
            w = self.last_w.get(wr)
            if w is not None and w not in deps:
                deps[w] = False
            for rd in self.readers.get(wr, ()):
                if rd not in deps:
                    deps[rd] = False
        if dma is not None:
            if dma in self.dma_last:
                deps[self.dma_last[dma]] = True
            else:
                self.dma_keys.append(dma)
            self.dma_last[dma] = op
        for d in extra_deps:
            deps[d] = True
        op.deps = []
        for d, raw in deps.items():
            if d is op:
                continue
            if d.dma is not None or op.dma is not None:
                need = True
            elif d.eng != op.eng:
                need = True
            else:
                need = (raw or STRICT_SYNC) and op.eng != "tensor"
            if need:
                op.deps.append(d)
                d.need_inc = True
        for r in reads:
            self.readers.setdefault(r, []).append(op)
        for wr in writes:
            self.last_w[wr] = op
            self.readers[wr] = []
        self.ops.append(op)
        self.last_eng[eng] = op
        return op

    def fence(self):
        lasts = list(self.last_eng.values()) + list(self.dma_last.values())
        for e in self.ENGS:
            self.add(e, lambda eng: eng.nop(), extra_deps=[d for d in lasts])

    def finalize(self, nc, es):
        self.eng_sem = {e: es.enter_context(nc.semaphore("s_" + e)) for e in self.ENGS}
        self.dma_sem = {k: es.enter_context(nc.semaphore("d_%d" % i)) for i, k in enumerate(self.dma_keys)}
        tick = {e: 0 for e in self.ENGS}
        dcount = {k: 0 for k in self.dma_keys}
        for op in self.ops:
            if op.dma is not None:
                dcount[op.dma] += 16
                op.tick = dcount[op.dma]
            elif op.need_inc:
                tick[op.eng] += 1
                op.tick = tick[op.eng]
        waited = {e: {} for e in self.ENGS}
        for op in self.ops:
            need = {}
            for d in op.deps:
                sem = self.dma_sem[d.dma] if d.dma is not None else self.eng_sem[d.eng]
                key = id(sem)
                if key not in need or need[key][1] < d.tick:
                    need[key] = (sem, d.tick)
            for key, (sem, val) in need.items():
                if waited[op.eng].get(key, 0) >= val:
                    continue
                waited[op.eng][key] = val
                op.waits.append((sem, val))

    def replay(self, ename, eng):
        for op in self.ops:
            if op.eng != ename:
                continue
            for sem, val in op.waits:
                eng.wait_ge(sem, val)
            inst = op.fn(eng)
            if op.dma is not None:
                inst.then_inc(self.dma_sem[op.dma], 16)
            elif op.need_inc:
                inst.then_inc(self.eng_sem[ename], 1)


def build_nc(debug=False, n_tiles_a=NT, n_tiles_b=NT):
    nc = bass.Bass("TRN2", target_bir_lowering=False)
    P = Prog()

    def din(name, shape, dt=F32):
        return nc.dram_tensor(name, list(shape), dt, kind="ExternalInput").ap()

    x_d = din("x", [S, D])
    rope_d = din("rope", [S, 64])
    g_ffn1_d = din("g_ffn1", [D]); g_mix_d = din("g_mix", [D]); g_ffn2_d = din("g_ffn2", [D]); g_final_d = din("g_final", [D])
    g_q_d = din("g_q", [64]); g_k_d = din("g_k", [64]); g_sgu_d = din("g_sgu", [512])
    g_ao_d = din("g_attn_out", [512]); g_so_d = din("g_sgu_out", [512])
    ws_d = din("w_s", [8, 128, 128]); bs_d = din("b_s", [8, 128])
    wf = {}
    for nm, shp in (("w1_gate", [D, DFF]), ("w1_up", [D, DFF]), ("w1_down", [DFF, D]), ("w_in", [D, DIN]),
                    ("w_out", [D, D]), ("w2_gate", [D, DFF]), ("w2_up", [D, DFF]), ("w2_down", [DFF, D])):
        wf[nm] = din(nm, shp)
    out_d = nc.dram_tensor("out", [S, D], F32, kind="ExternalOutput").ap()
    skind = "ExternalOutput" if debug else "Internal"
    wb = {nm: nc.dram_tensor(nm + "_bf", list(wf[nm].shape), BF16, kind="Internal").ap() for nm in wf}
    x1s_d = nc.dram_tensor("x1s", [S, D], F32, kind=skind).ap()
    qts_d = nc.dram_tensor("qts", [4, 128, S], BF16, kind=skind).ap()
    if debug:
        kt_dbg = nc.dram_tensor("kt_dbg", [128, S], BF16, kind="ExternalOutput").ap()
        va_dbg = nc.dram_tensor("va_dbg", [128, NKT * 2 * 65], BF16, kind="ExternalOutput").ap()
        attn_dbg = nc.dram_tensor("attn_dbg", [S, 512], F32, kind="ExternalOutput").ap()
        x2_dbg = nc.dram_tensor("x2_dbg", [S, D], F32, kind="ExternalOutput").ap()
        x3_dbg = nc.dram_tensor("x3_dbg", [S, D], F32, kind="ExternalOutput").ap()

    def tiled(ap):
        return ap.rearrange("(n s p) d -> n p s d", s=4, p=128)

    with ExitStack() as es:
        def sb(name, shape, dt):
            return es.enter_context(nc.sbuf_tensor(name, list(shape), dt))

        ident_b = sb("ident_b", [128, 128], BF16)
        ident_f = sb("ident_f", [128, 128], F32)
        g1 = sb("g1", [128, D], F32)
        g2 = sb("g2", [128, D], F32)
        g3 = sb("g3", [128, 512], F32)
        g4 = sb("g4", [128, 512], F32)
        gq = sb("gq", [128, 64], F32)
        gk = sb("gk", [128, 64], F32)
        KT = sb("KT", [128, S], BF16)
        VA = sb("VA", [128, NKT, 2, 65], BF16)
        WsT = sb("WsT", [128, 8, 128], BF16)
        bsT = sb("bsT", [128, 8], F32)
        mhalf = sb("mhalf", [128, 16], F32)
        ring = [sb("ring%d" % i, [128, SLOT], BF16) for i in range(RING)]
        xt = [sb("xt%d" % i, [128, 4, D], F32) for i in range(2)]
        xn = [sb("xn%d" % i, [128, D], BF16) for i in range(2)]
        xnT = sb("xnT", [128, 8, TT], BF16)
        hnT = sb("hnT", [128, 8, TT], BF16)
        hT = sb("hT", [128, NFT, TT], BF16)
        sg = [sb("sg%d" % i, [128, TT], F32) for i in range(2)]
        cs = [sb("cs%d" % i, [128, 4, 64], F32) for i in range(2)]
        QTs = [sb("QTs%d" % i, [128, 4, TT], BF16) for i in range(2)]
        msT = sb("msT", [128, 4, TT], BF16)
        stats = sb("stats", [128, 64], F32)
        arena_f = sb("arena_f", [128, 7296], F32)
        arena_b = sb("arena_b", [128, 3072], BF16)
        ps = [es.enter_context(nc.psum_tensor("ps%d" % i, [128, 2, 512], F32)) for i in range(4)]

        def bank(i):
            return ps[i // 2][:, i % 2, :]

        def bank_bf(i):
            return ps[i // 2][:, i % 2, :].bitcast(BF16)

        def PSK(i):
            return ("ps", i)

        o = [0]

        def carve(n):
            a = arena_f[:, o[0]:o[0] + n]
            o[0] += n
            return a
        qsb = carve(512); sq = carve(512); ksb = carve(128); zsq = carve(512); zt = carve(512)
        u2 = carve(2048); v2 = carve(512); ftmp = carve(512); sgu = carve(512)
        rt1 = carve(256); rt2 = carve(256); rt3 = carve(256); rt4 = carve(256)
        assert o[0] <= 7296, o[0]
        vvh = arena_b[:, 0:512]; ms = arena_b[:, 512:1024]; qrot = arena_b[:, 1024:1536]; krot = arena_b[:, 1536:1664]
        Osb = arena_f[:, 0:1024].rearrange("p (h t) -> p h t", h=2)
        attn_tm = arena_f[:, 1024:3072].rearrange("p (s c) -> p s c", s=4)
        Pbuf = [arena_b[:, i * 1024:(i + 1) * 1024] for i in range(3)]

        st_ssq = stats[:, 0:4]; st_ms = stats[:, 4:8]; st_rstd = stats[:, 8:12]
        st_q = stats[:, 12:22]; st_qm = stats[:, 22:32]; st_qr = stats[:, 32:42]
        st_v = stats[:, 42:43]; st_vm = stats[:, 43:44]; st_vr = stats[:, 44:45]
        st_s = stats[:, 45:46]; st_sm = stats[:, 46:47]; st_sr = stats[:, 47:48]
        st_rc = stats[:, 48:56]

        stats2 = sb("stats2", [128, 16], F32)
        st_sets = [(st_ssq, st_ms, st_rstd), (stats2[:, 0:4], stats2[:, 4:8], stats2[:, 8:12])]

        def xk(b, sub):
            return [("xt", b, sub, q) for q in range(4)]

        def flat2048(ap):
            n = ap.shape[0] * ap.shape[1] // 2048
            return ap.rearrange("a b -> (a b)").rearrange("(n c) -> n c", c=2048), n

        def cast_weight(nm):
            src, _ = flat2048(wf[nm]); dst, _ = flat2048(wb[nm])
            P.add("gpsimd", lambda e: e.dma_start(out=dst, in_=src), writes=[("wb", nm)], dma=("cast", nm))

        P.add("gpsimd", lambda e: e.memset(ident_b[:], 0.0), writes=[("ident_b",)])
        P.add("gpsimd", lambda e: e.affine_select(out=ident_b[:], in_=ident_b[:], pattern=[[-1, 128]], compare_op=ALU.not_equal,
                                                   fill=1.0, base=0, channel_multiplier=1), reads=[("ident_b",)], writes=[("ident_b",)])
        P.add("gpsimd", lambda e: e.memset(ident_f[:], 0.0), writes=[("ident_f",)])
        P.add("gpsimd", lambda e: e.affine_select(out=ident_f[:], in_=ident_f[:], pattern=[[-1, 128]], compare_op=ALU.not_equal,
                                                   fill=1.0, base=0, channel_multiplier=1), reads=[("ident_f",)], writes=[("ident_f",)])
        P.add("gpsimd", lambda e: e.memset(mhalf[:], -0.5), writes=[("mhalf",)])
        P.add("gpsimd", lambda e: e.memset(VA[:].rearrange("p a b c -> p (a b) c")[:, :, 64:65], 1.0), writes=[("VA1",)])
        for nm in ("w1_gate", "w1_up", "w1_down", "w_in", "w_out"):
            cast_weight(nm)

        def load_gain(dst, src, key):
            P.add("sync", lambda e: e.dma_start(out=dst[:], in_=src.partition_broadcast(128)), writes=[key], dma=("g",) + key)
        load_gain(g1, g_ffn1_d, ("g1",)); load_gain(g2, g_mix_d, ("g2",)); load_gain(g3, g_sgu_d, ("g3",))
        load_gain(g4, g_so_d, ("g4",)); load_gain(gq, g_q_d, ("gq",)); load_gain(gk, g_k_d, ("gk",))

        def ld_bs(e):
            with nc.allow_non_contiguous_dma(reason="tiny b_s transpose"):
                return e.dma_start(out=bsT[:], in_=bs_d.rearrange("g p -> p g"))
        P.add("sync", ld_bs, writes=[("bsT",)], dma=("bsT",))
        ws_f = arena_f[:, 0:1024].rearrange("p (g q) -> p g q", g=8)
        ws_b = arena_b[:, 0:1024].rearrange("p (g q) -> p g q", g=8)
        P.add("sync", lambda e: e.dma_start(out=ws_f, in_=ws_d.rearrange("g p q -> p g q")), writes=[("ws_f",)], dma=("ws_f",))
        P.add("vector", lambda e: e.tensor_copy(out=ws_b, in_=ws_f), reads=[("ws_f",)], writes=[("ws_b",)])

        def ws_tr(e):
            for g in range(8):
                i = e.transpose(out=bank_bf(6)[:, g * 128:(g + 1) * 128], in_=ws_b[:, g, :], identity=ident_b[:])
            return i
        P.add("tensor", ws_tr, reads=[("ws_b",), ("ident_b",)], writes=[PSK(6)])
        P.add("vector", lambda e: e.tensor_copy(out=WsT[:].rearrange("p g q -> p (g q)"), in_=bank_bf(6)), reads=[PSK(6)], writes=[("WsT",)])
        P.fence()

        stage_ctr = {}

        def stage_load(kind, idx, names, slots=(0, 1, 2)):
            n_ = stage_ctr.get(slots, 0)
            stage_ctr[slots] = n_ + 1
            slot = slots[n_ % len(slots)]
            r = ring[slot]
            key = ("ring", slot)
            key2 = ("ring2", slot)
            if kind == "U":
                wg = wb[names[0]].rearrange("(kt p) n -> p kt n", p=128)[:, :, idx * 256:(idx + 1) * 256]
                wu = wb[names[1]].rearrange("(kt p) n -> p kt n", p=128)[:, :, idx * 256:(idx + 1) * 256]
                P.add("sync", lambda e: e.dma_start(out=r[:, 0:2048].rearrange("p (k n) -> p k n", k=8), in_=wg),
                      reads=[("wb", names[0])], writes=[key], dma=("ringd", slot))
                P.add("sync", lambda e: e.dma_start(out=r[:, 2048:4096].rearrange("p (k n) -> p k n", k=8), in_=wu),
                      reads=[("wb", names[1])], writes=[key2], dma=("ringd2", slot))
            elif kind == "D":
                wd = wb[names[0]].rearrange("(ft p) n -> p ft n", p=128)[:, :, idx * 256:(idx + 1) * 256]
                P.add("sync", lambda e: e.dma_start(out=r[:, 0:5632].rearrange("p (f n) -> p f n", f=NFT), in_=wd),
                      reads=[("wb", names[0])], writes=[key, key2], dma=("ringd", slot))
            elif kind == "IN":
                c0, w = idx
                wi = wb["w_in"].rearrange("(kt p) n -> p kt n", p=128)[:, :, c0:c0 + w]
                P.add("sync", lambda e: e.dma_start(out=r[:, 0:8 * w].rearrange("p (k n) -> p k n", k=8), in_=wi),
                      reads=[("wb", "w_in")], writes=[key, key2], dma=("ringd", slot))
            elif kind == "OUT":
                wo = wb["w_out"].rearrange("(kt p) n -> p kt n", p=128)[:, idx * 4:(idx + 1) * 4, :]
                P.add("sync", lambda e: e.dma_start(out=r[:, 0:4096].rearrange("p (k n) -> p k n", k=4), in_=wo),
                      reads=[("wb", "w_out")], writes=[key, key2], dma=("ringd", slot))
            return slot

        def emit_rstd(ssq_ap, tmp_ap, rstd_ap, n, scale, eps, keys):
            ks, km, kr = keys
            P.add("vector", lambda e: e.tensor_scalar(out=tmp_ap, in0=ssq_ap, scalar1=scale, scalar2=eps, op0=ALU.mult, op1=ALU.add),
                  reads=[ks], writes=[km])
            P.add("scalar", lambda e: e.activation(out=tmp_ap, in_=tmp_ap, func=AF.Sqrt), reads=[km], writes=[km])
            P.add("vector", lambda e: e.reciprocal(out=rstd_ap, in_=tmp_ap), reads=[km], writes=[kr])

        def drain(g):
            for _ in g:
                pass

        def emit_norm_T(b, gain, gkey, dstT, dkey, tb=6, st=0, xp=None):
            ssq_, ms_, rstd_ = st_sets[st]
            if xp is None:
                xp = ((xn[0][:], [("xn", 0)]), (xn[1][:], [("xn", 1)]))
            for sub in range(4):
                P.add("scalar", lambda e, sub=sub: e.activation(out=xp[sub % 2][0], in_=xt[b][:, sub, :], func=AF.Square,
                                                                 accum_out=ssq_[:, sub:sub + 1]),
                      reads=xk(b, sub), writes=xp[sub % 2][1] + [("st_ssq", st, sub)])
            yield 0
            P.add("vector", lambda e: e.tensor_scalar(out=ms_, in0=ssq_, scalar1=1.0 / D, scalar2=EPS, op0=ALU.mult, op1=ALU.add),
                  reads=[("st_ssq", st, s_) for s_ in range(4)], writes=[("st_ms", st)])
            P.add("scalar", lambda e: e.activation(out=ms_, in_=ms_, func=AF.Sqrt), reads=[("st_ms", st)], writes=[("st_ms", st)])
            P.add("vector", lambda e: e.reciprocal(out=rstd_, in_=ms_), reads=[("st_ms", st)], writes=[("st_rstd", st)])

            def normalize(sub):
                xb, xkeys = xp[sub % 2]
                P.add("vector", lambda e: e.scalar_tensor_tensor(out=xb, in0=xt[b][:, sub, :], scalar=rstd_[:, sub:sub + 1],
                                                                 in1=gain[:], op0=ALU.mult, op1=ALU.mult),
                      reads=xk(b, sub) + [("st_rstd", st), gkey], writes=xkeys)

            def transp(sub):
                xb, xkeys = xp[sub % 2]

                def tr(e):
                    for k in range(8):
                        i = e.transpose(out=bank_bf(tb)[:, k * 128:(k + 1) * 128], in_=xb[:, k * 128:(k + 1) * 128], identity=ident_b[:])
                    return i
                P.add("tensor", tr, reads=xkeys + [("ident_b",)], writes=[PSK(tb)])
                P.add("scalar", lambda e: e.activation(out=dstT[:, :, sub * 128:(sub + 1) * 128],
                                                       in_=bank_bf(tb).rearrange("p (k t) -> p k t", k=8), func=AF.Copy),
                      reads=[PSK(tb)], writes=[(dkey, sub)])
            normalize(0)
            yield 0
            for sub in range(4):
                transp(sub)
                if sub + 1 < 4:
                    normalize(sub + 1)
                yield 0

        def emit_ffn(b, names, gbs=(0, 1), ubs=(2, 3), ybs=(4, 5), use_tanh=False, slots=(0, 1, 2), split=1, dsplit=1):
            for j in range(11):
                slot = stage_load("U", j, names[0:2], slots)
                r = ring[slot]
                for t in range(2):
                    f = 2 * j + t
                    gb, ub = gbs[f % len(gbs)], ubs[f % len(ubs)]

                    def mmg(e, r=r, t=t, gb=gb):
                        for k in range(8):
                            i = e.matmul(bank(gb), lhsT=r[:, k * 256 + t * 128:k * 256 + (t + 1) * 128], rhs=xnT[:, k, :], start=(k == 0), stop=(k == 7))
                        return i

                    def mmu(e, r=r, t=t, ub=ub):
                        for k in range(8):
                            i = e.matmul(bank(ub), lhsT=r[:, 2048 + k * 256 + t * 128:2048 + k * 256 + (t + 1) * 128], rhs=xnT[:, k, :],
                                         start=(k == 0), stop=(k == 7))
                        return i
                    if use_tanh:
                        kc = 8 // split
                        for c_ in range(split):
                            def mmg_c(e, r=r, t=t, gb=gb, c_=c_):
                                for k in range(c_ * kc, (c_ + 1) * kc):
                                    i = e.matmul(bank(gb), lhsT=r[:, k * 256 + t * 128:k * 256 + (t + 1) * 128], rhs=xnT[:, k, :], start=(k == 0), stop=(k == 7))
                                return i
                            P.add("tensor", mmg_c, reads=[("ring", slot)] + [("xnT", s_) for s_ in range(4)], writes=[PSK(gb)])
                            if c_ == split - 1:
                                P.add("scalar", lambda e, f=f, gb=gb: e.activation(out=sg[f % 2][:], in_=bank(gb), func=AF.Tanh, scale=0.5),
                                      reads=[PSK(gb)], writes=[("sg", f % 2)])
                                P.add("vector", lambda e, f=f, gb=gb: e.scalar_tensor_tensor(out=sg[f % 2][:], in0=sg[f % 2][:], scalar=1.0, in1=bank(gb),
                                                                                          op0=ALU.add, op1=ALU.mult),
                                      reads=[("sg", f % 2), PSK(gb)], writes=[("sg", f % 2)])
                            yield
                        for c_ in range(split):
                            def mmu_c(e, r=r, t=t, ub=ub, c_=c_):
                                for k in range(c_ * kc, (c_ + 1) * kc):
                                    i = e.matmul(bank(ub), lhsT=r[:, 2048 + k * 256 + t * 128:2048 + k * 256 + (t + 1) * 128], rhs=xnT[:, k, :],
                                                 start=(k == 0), stop=(k == 7))
                                return i
                            P.add("tensor", mmu_c, reads=[("ring2", slot)] + [("xnT", s_) for s_ in range(4)], writes=[PSK(ub)])
                            if c_ == split - 1:
                                P.add("vector", lambda e, f=f, ub=ub: e.scalar_tensor_tensor(out=hT[:, f, :], in0=sg[f % 2][:], scalar=0.5, in1=bank(ub),
                                                                                          op0=ALU.mult, op1=ALU.mult),
                                      reads=[("sg", f % 2), PSK(ub)], writes=[("hT", f)])
                            yield
                    else:
                        P.add("tensor", mmg, reads=[("ring", slot)] + [("xnT", s_) for s_ in range(4)], writes=[PSK(gb)])
                        P.add("scalar", lambda e, f=f, gb=gb: e.activation(out=sg[f % 2][:], in_=bank(gb), func=AF.Silu),
                              reads=[PSK(gb)], writes=[("sg", f % 2)])
                        P.add("tensor", mmu, reads=[("ring2", slot)] + [("xnT", s_) for s_ in range(4)], writes=[PSK(ub)])
                        P.add("vector", lambda e, f=f, ub=ub: e.tensor_tensor(out=hT[:, f, :], in0=sg[f % 2][:], in1=bank(ub), op=ALU.mult),
                              reads=[("sg", f % 2), PSK(ub)], writes=[("hT", f)])
                        yield
            n = 0
            for qd in range(4):
                slot = stage_load("D", qd, names[2:3], slots)
                r = ring[slot]
                for sub in range(4):
                    yb = ybs[n % len(ybs)]
                    n += 1

                    bounds = [(NFT * c_) // dsplit for c_ in range(dsplit + 1)]
                    for c_ in range(dsplit):
                        def mm(e, r=r, sub=sub, yb=yb, lo=bounds[c_], hi=bounds[c_ + 1]):
                            for f in range(lo, hi):
                                i = e.matmul(bank(yb)[:, 0:256], lhsT=hT[:, f, sub * 128:(sub + 1) * 128], rhs=r[:, f * 256:(f + 1) * 256],
                                             start=(f == 0), stop=(f == NFT - 1))
                            return i
                        P.add("tensor", mm, reads=[("ring", slot), ("ring2", slot)] + [("hT", f) for f in range(NFT)], writes=[PSK(yb)])
                        if c_ == dsplit - 1:
                            P.add("vector", lambda e, sub=sub, qd=qd, yb=yb: e.scalar_tensor_tensor(
                                out=xt[b][:, sub, qd * 256:(qd + 1) * 256], in0=bank(yb)[:, 0:256], scalar=0.5,
                                in1=xt[b][:, sub, qd * 256:(qd + 1) * 256], op0=ALU.mult, op1=ALU.add),
                                reads=[PSK(yb), ("xt", b, sub, qd)], writes=[("xt", b, sub, qd)])
                        yield

        def emit_outproj(b, src_T, skey, half, ybs=(4, 5), slots=(0, 1, 2)):
            slot = stage_load("OUT", half, None, slots)
            r = ring[slot]
            n = 0
            for sub in range(4):
                for hf in range(2):
                    yb = ybs[n % len(ybs)]
                    n += 1

                    def mm(e, r=r, sub=sub, hf=hf, yb=yb):
                        for k in range(4):
                            i = e.matmul(bank(yb), lhsT=src_T[:, k, sub * 128:(sub + 1) * 128], rhs=r[:, k * 1024 + hf * 512:k * 1024 + (hf + 1) * 512],
                                         start=(k == 0), stop=(k == 3))
                        return i
                    P.add("tensor", mm, reads=[("ring", slot), ("ring2", slot), (skey, sub)], writes=[PSK(yb)])
                    P.add("vector", lambda e, sub=sub, hf=hf, yb=yb: e.tensor_tensor(
                        out=xt[b][:, sub, hf * 512:(hf + 1) * 512], in0=bank(yb), in1=xt[b][:, sub, hf * 512:(hf + 1) * 512], op=ALU.add),
                        reads=[PSK(yb), ("xt", b, sub, 2 * hf), ("xt", b, sub, 2 * hf + 1)],
                        writes=[("xt", b, sub, 2 * hf), ("xt", b, sub, 2 * hf + 1)])
                    yield

        def emit_qk(src_bank, nh, sbuf_f, gain, gkey, csb, sub, rot_out_view, rkey, tagk):
            w = nh * 64
            srcp = bank(src_bank)[:, 0:w] if isinstance(src_bank, int) else src_bank
            P.add("scalar", lambda e: e.activation(out=sq[:, 0:w], in_=srcp, func=AF.Square), reads=[tagk], writes=[("sq",)])
            P.add("scalar", lambda e: e.activation(out=sbuf_f[:, 0:w], in_=srcp, func=AF.Copy), reads=[tagk], writes=[("qk_f",)])
            P.add("vector", lambda e: e.tensor_reduce(out=st_q[:, 0:nh], in_=sq[:, 0:w].rearrange("p (h d) -> p h d", h=nh), op=ALU.add, axis=AX.X),
                  reads=[("sq",)], writes=[("st_q",)])
            emit_rstd(st_q[:, 0:nh], st_qm[:, 0:nh], st_qr[:, 0:nh], nh, 1.0 / 64, EPS, (("st_q",), ("st_qm",), ("st_qr",)))
            v3 = sbuf_f[:, 0:w].rearrange("p (h d) -> p h d", h=nh)
            P.add("vector", lambda e: e.tensor_tensor(out=v3, in0=v3, in1=st_qr[:, 0:nh].unsqueeze(2).to_broadcast([128, nh, 64]), op=ALU.mult),
                  reads=[("qk_f",), ("st_qr",)], writes=[("qk_f",)])
            P.add("vector", lambda e: e.tensor_tensor(out=v3, in0=v3, in1=gain[:].unsqueeze(1).to_broadcast([128, nh, 64]), op=ALU.mult),
                  reads=[("qk_f",), gkey], writes=[("qk_f",)])
            v4 = sbuf_f[:, 0:w].rearrange("p (h i t) -> p h i t", h=nh, t=2)
            x1 = v4[:, :, :, 0]
            x2 = v4[:, :, :, 1]
            cosb = csb[:, sub, 0:32].unsqueeze(1).to_broadcast([128, nh, 32])
            sinb = csb[:, sub, 32:64].unsqueeze(1).to_broadcast([128, nh, 32])
            hw = nh * 32
            t1 = rt1[:, 0:hw].rearrange("p (h i) -> p h i", h=nh); t2 = rt2[:, 0:hw].rearrange("p (h i) -> p h i", h=nh)
            t3 = rt3[:, 0:hw].rearrange("p (h i) -> p h i", h=nh); t4 = rt4[:, 0:hw].rearrange("p (h i) -> p h i", h=nh)
            ck = ("cs", id(csb))
            P.add("vector", lambda e: e.tensor_tensor(out=t1, in0=x1, in1=cosb, op=ALU.mult), reads=[("qk_f",), ck], writes=[("rt1",)])
            P.add("vector", lambda e: e.tensor_tensor(out=t2, in0=x2, in1=sinb, op=ALU.mult), reads=[("qk_f",), ck], writes=[("rt2",)])
            P.add("vector", lambda e: e.tensor_tensor(out=t3, in0=x1, in1=sinb, op=ALU.mult), reads=[("qk_f",), ck], writes=[("rt3",)])
            P.add("vector", lambda e: e.tensor_tensor(out=t4, in0=x2, in1=cosb, op=ALU.mult), reads=[("qk_f",), ck], writes=[("rt4",)])
            P.add("vector", lambda e: e.tensor_tensor(out=rot_out_view(0), in0=rot_in_view(t1, nh), in1=rot_in_view(t2, nh), op=ALU.subtract),
                  reads=[("rt1",), ("rt2",)], writes=[rkey])
            P.add("vector", lambda e: e.tensor_tensor(out=rot_out_view(1), in0=rot_in_view(t3, nh), in1=rot_in_view(t4, nh), op=ALU.add),
                  reads=[("rt3",), ("rt4",)], writes=[rkey])

        def rot_in_view(t, nh):
            if nh == 8:
                return t.rearrange("p (a b) i -> p a b i", a=2)
            return t

        def q_out_view(tt):
            return qrot.rearrange("p (b a i t) -> p a b i t", b=4, a=2, t=2)[:, :, :, :, tt]

        def k_out_view(tt):
            return krot.rearrange("p (h i t) -> p h i t", h=2, t=2)[:, :, :, tt]

        def load_tile_a(i):
            b = i % 2
            P.add("gpsimd", lambda e: e.dma_start(out=xt[b][:], in_=tiled(x_d)[i]), writes=[k_ for s_ in range(4) for k_ in xk(b, s_)], dma=("xld", b))
            P.add("gpsimd", lambda e: e.dma_start(out=cs[b][:], in_=tiled(rope_d)[i]), writes=[("cs", id(cs[b]))], dma=("csld", b))

        def main_gen(i):
            b = i % 2
            yield from emit_norm_T(b, g1, ("g1",), xnT, "xnT", tb=4, st=0)
            yield from emit_ffn(b, ("w1_gate", "w1_up", "w1_down"), gbs=(0, 1), ubs=(2, 3), ybs=(4,), slots=(0, 1))

        def bg_gen(i):
            b = i % 2
            csb = cs[b]
            BS = (2,)
            xpb = ((arena_b[:, 0:1024], [("vvh",), ("ms",)]), (arena_b[:, 2048:3072], [("xnb1",)]))
            junk_b = arena_b[:, 2048:2560]
            yield from emit_norm_T(b, g2, ("g2",), hnT, "hnT", tb=7, st=1, xp=xpb)

            def proj(slot, sub, w, pb):
                r = ring[slot]

                def mm(e):
                    for k in range(8):
                        i_ = e.matmul(bank(pb)[:, 0:w], lhsT=hnT[:, k, sub * 128:(sub + 1) * 128], rhs=r[:, k * w:(k + 1) * w], start=(k == 0), stop=(k == 7))
                    return i_
                P.add("tensor", mm, reads=[("ring", slot), ("ring2", slot), ("hnT", sub)], writes=[PSK(pb)])

            def q_finish(sub):
                def trq(e):
                    for pr in range(4):
                        i_ = e.transpose(out=bank_bf(7)[:, pr * 128:(pr + 1) * 128], in_=qrot[:, pr * 128:(pr + 1) * 128], identity=ident_b[:])
                    return i_
                P.add("tensor", trq, reads=[("qrot",), ("ident_b",)], writes=[PSK(7)])
                P.add("vector", lambda e: e.tensor_copy(out=QTs[0][:, :, sub * 128:(sub + 1) * 128],
                                                        in_=bank_bf(7)[:, 0:512].rearrange("p (a t) -> p a t", a=4)),
                      reads=[PSK(7)], writes=[("QTs", 0, sub)])
            slot = stage_load("IN", (0, 512), None, BS)
            for sub in range(4):
                pb = 5 + sub % 2
                proj(slot, sub, 512, pb)
                if sub > 0:
                    q_finish(sub - 1)
                yield 0
                emit_qk(pb, 8, qsb, gq, ("gq",), csb, sub, q_out_view, ("qrot",), PSK(pb))
                yield BG_DELAY
            q_finish(3)
            P.add("gpsimd", lambda e: e.dma_start(out=qts_d.rearrange("a p t -> p a t")[:, :, i * TT:(i + 1) * TT], in_=QTs[0][:]),
                  reads=[("QTs", 0, s_) for s_ in range(4)], writes=[("qts", i)], dma=("qst",))
            def k_finish(sub):
                P.add("tensor", lambda e: e.transpose(out=bank_bf(7)[:, 0:128], in_=krot[:, 0:128], identity=ident_b[:]),
                      reads=[("krot",), ("ident_b",)], writes=[PSK(7)])
                tg = i * 4 + sub
                P.add("vector", lambda e: e.tensor_copy(out=KT[:, tg * 128:(tg + 1) * 128], in_=bank_bf(7)[:, 0:128]),
                      reads=[PSK(7)], writes=[("KT", tg)])
            slot = stage_load("IN", (512, 256), None, BS)
            for sub in range(4):
                pb = 5 + sub % 2
                proj(slot, sub, 256, pb)
                if sub > 0:
                    k_finish(sub - 1)
                yield 0
                tg = i * 4 + sub
                P.add("scalar", lambda e, tg=tg, pb=pb: e.activation(out=VA[:, tg, :, 0:64], in_=bank(pb)[:, 128:256].rearrange("p (h d) -> p h d", h=2),
                                                                     func=AF.Copy),
                      reads=[PSK(pb), ("VA1",)], writes=[("VA", tg)])
                emit_qk(bank(pb)[:, 0:128], 2, ksb, gk, ("gk",), csb, sub, k_out_view, ("krot",), PSK(pb))
                yield BG_DELAY
            k_finish(3)

            def emit_gelu2(pb, dst, dkey):
                src = bank(pb)
                P.add("scalar", lambda e: e.activation(out=zsq, in_=src, func=AF.Square), reads=[PSK(pb)], writes=[("zsq",)])
                P.add("vector", lambda e: e.tensor_scalar(out=zsq, in0=zsq, scalar1=0.044715, scalar2=1.0, op0=ALU.mult, op1=ALU.add),
                      reads=[("zsq",)], writes=[("zsq",)])
                P.add("vector", lambda e: e.tensor_tensor(out=zt, in0=zsq, in1=src, op=ALU.mult), reads=[("zsq",), PSK(pb)], writes=[("zt",)])
                P.add("scalar", lambda e: e.activation(out=zt, in_=zt, func=AF.Tanh, scale=GELU_C), reads=[("zt",)], writes=[("zt",)])
                P.add("vector", lambda e: e.scalar_tensor_tensor(out=dst, in0=zt, scalar=1.0, in1=src, op0=ALU.add, op1=ALU.mult),
                      reads=[("zt",), PSK(pb)], writes=[dkey])

            slot = stage_load("IN", (768, 512), None, BS)
            for sub in range(4):
                pb = 5 + sub % 2
                proj(slot, sub, 512, pb)
                yield 0
                emit_gelu2(pb, u2[:, sub * 512:(sub + 1) * 512], ("u2", sub))
                yield 0
            def zv_spat(sub):
                def spat(e):
                    for g in range(8):
                        i_ = e.matmul(bank(7)[:, g * 64:(g + 1) * 64], lhsT=WsT[:, g, :], rhs=vvh[:, g * 64:(g + 1) * 64], start=True, stop=True)
                    return i_
                P.add("tensor", spat, reads=[("vvh",), ("WsT",)], writes=[PSK(7)])
                P.add("vector", lambda e: e.tensor_tensor(out=ftmp, in0=bank(7), in1=g3[:], op=ALU.mult), reads=[PSK(7), ("g3",)], writes=[("ftmp",)])
                f3 = ftmp.rearrange("p (g d) -> p g d", g=8)
                P.add("vector", lambda e: e.tensor_tensor(out=f3, in0=f3, in1=bsT[:].unsqueeze(2).to_broadcast([128, 8, 64]), op=ALU.add),
                      reads=[("ftmp",), ("bsT",)], writes=[("ftmp",)])
                P.add("vector", lambda e: e.scalar_tensor_tensor(out=sgu, in0=ftmp, scalar=0.5, in1=u2[:, sub * 512:(sub + 1) * 512],
                                                                 op0=ALU.mult, op1=ALU.mult),
                      reads=[("ftmp",), ("u2", sub)], writes=[("sgu",)])
                P.add("scalar", lambda e: e.activation(out=junk_b, in_=sgu, func=AF.Square, accum_out=st_s), reads=[("sgu",)], writes=[("xnb1",), ("st_s",)])
                emit_rstd(st_s, st_sm, st_sr, 1, 1.0 / 512, EPS, (("st_s",), ("st_sm",), ("st_sr",)))
                P.add("vector", lambda e: e.scalar_tensor_tensor(out=ms, in0=sgu, scalar=st_sr, in1=g4[:], op0=ALU.mult, op1=ALU.mult),
                      reads=[("sgu",), ("st_sr",), ("g4",)], writes=[("ms",)])

            def zv_finish(sub):
                def trm(e):
                    for k in range(4):
                        i_ = e.transpose(out=bank_bf(7)[:, k * 128:(k + 1) * 128], in_=ms[:, k * 128:(k + 1) * 128], identity=ident_b[:])
                    return i_
                P.add("tensor", trm, reads=[("ms",), ("ident_b",)], writes=[PSK(7)])
                P.add("scalar", lambda e: e.activation(out=msT[:, :, sub * 128:(sub + 1) * 128],
                                                       in_=bank_bf(7)[:, 0:512].rearrange("p (a t) -> p a t", a=4), func=AF.Copy),
                      reads=[PSK(7)], writes=[("msT", sub)])
            slot = stage_load("IN", (1280, 512), None, BS)
            for sub in range(4):
                pb = 5 + sub % 2
                proj(slot, sub, 512, pb)
                yield 0
                emit_gelu2(pb, v2, ("v2",))
                P.add("scalar", lambda e: e.activation(out=junk_b, in_=v2, func=AF.Square, accum_out=st_v), reads=[("v2",)], writes=[("xnb1",), ("st_v",)])
                emit_rstd(st_v, st_vm, st_vr, 1, 1.0 / 512, 4 * EPS, (("st_v",), ("st_vm",), ("st_vr",)))
                P.add("vector", lambda e: e.tensor_scalar(out=vvh, in0=v2, scalar1=st_vr, scalar2=None, op0=ALU.mult), reads=[("v2",), ("st_vr",)], writes=[("vvh",)])
                yield BG_DELAY
                zv_spat(sub)
                yield BG_DELAY
                zv_finish(sub)
                yield 0
            yield from emit_outproj(b, msT, "msT", 1, ybs=(5, 6), slots=BS)
            P.add("gpsimd", lambda e: e.dma_start(out=tiled(x1s_d)[i], in_=xt[b][:]),
                  reads=[k_ for s_ in range(4) for k_ in xk(b, s_)], writes=[("x1s", i)], dma=("xst", b))
            yield 0

        if n_tiles_a > 0:
            load_tile_a(0)
            if n_tiles_a > 1:
                load_tile_a(1)
            drain(main_gen(0))
        for i in range(n_tiles_a):
            bg = bg_gen(i)
            main = main_gen(i + 1) if i + 1 < n_tiles_a else None
            bg_done = False
            wait = 0
            if main is not None:
                for _ in main:
                    if bg_done:
                        continue
                    if wait > 0:
                        wait -= 1
                        continue
                    try:
                        wait = next(bg) or 0
                    except StopIteration:
                        bg_done = True
                        if i + 2 < n_tiles_a:
                            load_tile_a(i + 2)
            if not bg_done:
                drain(bg)
                if i + 2 < n_tiles_a:
                    load_tile_a(i + 2)
            if i == min(2, n_tiles_a - 1):
                for nm in ("w2_gate", "w2_up", "w2_down"):
                    cast_weight(nm)

        if debug:
            P.add("gpsimd", lambda e: e.dma_start(out=kt_dbg, in_=KT[:]), reads=[("KT", t_) for t_ in range(NKT)], dma=("dbg1",))
            P.add("gpsimd", lambda e: e.dma_start(out=va_dbg, in_=VA[:].rearrange("p a b c -> p (a b c)")),
                  reads=[("VA", t_) for t_ in range(NKT)] + [("VA1",)], dma=("dbg2",))
        P.fence()

        load_gain(g1, g_ffn2_d, ("g1",)); load_gain(g2, g_final_d, ("g2",)); load_gain(g3, g_ao_d, ("g3",))

        def load_x_b(c):
            b = c % 2
            P.add("gpsimd", lambda e: e.dma_start(out=xt[b][:], in_=tiled(x1s_d)[c]), reads=[("x1s", c)],
                  writes=[k_ for s_ in range(4) for k_ in xk(b, s_)], dma=("xld", b))

        def load_q_b(c):
            b = c % 2
            P.add("gpsimd", lambda e: e.dma_start(out=QTs[b][:], in_=qts_d.rearrange("a p t -> p a t")[:, :, c * TT:(c + 1) * TT]),
                  reads=[("qts", c)], writes=[("QTs", b, s_) for s_ in range(4)], dma=("qld", b))

        def attention_gen(c):
            b = c % 2
            Q = QTs[b]

            def emit_qk_mm(n):
                pr, kt = divmod(n, NKT)
                sp = n % 2

                def qk(e):
                    e.matmul(bank(2 * sp), lhsT=KT[0:64, kt * 128:(kt + 1) * 128], rhs=Q[0:64, pr, :], start=True, stop=True)
                    return e.matmul(bank(2 * sp + 1), lhsT=KT[64:128, kt * 128:(kt + 1) * 128], rhs=Q[64:128, pr, :], start=True, stop=True)
                P.add("tensor", qk, reads=[("KT", kt)] + [("QTs", b, s_) for s_ in range(4)], writes=[PSK(2 * sp), PSK(2 * sp + 1)])

            def emit_o_post(pr):
                def tro(e):
                    for h in range(2):
                        for sub in range(4):
                            i_ = e.transpose(out=bank(4 + h)[:, sub * 66:sub * 66 + 65], in_=Osb[0:65, h, sub * 128:(sub + 1) * 128], identity=ident_f[0:65, 0:65])
                    return i_
                P.add("tensor", tro, reads=[("Osb",), ("ident_f",)], writes=[PSK(4), PSK(5)])
                ov = ps[2][:, :, 0:264].rearrange("p h (s c) -> p h s c", c=66)
                P.add("vector", lambda e: e.reciprocal(out=st_rc.rearrange("p (h s) -> p h s", h=2), in_=ov[:, :, :, 64]),
                      reads=[PSK(4), PSK(5)], writes=[("st_rc",)])
                for h in range(2):
                    hd = pr + 4 * h
                    P.add("vector", lambda e, h=h, hd=hd: e.tensor_tensor(
                        out=attn_tm[:, :, hd * 64:(hd + 1) * 64], in0=ov[:, h, :, 0:64],
                        in1=st_rc[:, h * 4:(h + 1) * 4].unsqueeze(2).to_broadcast([128, 4, 64]), op=ALU.mult),
                        reads=[PSK(4 + h), ("st_rc",)], writes=[("attn_tm", hd)])

            NIT = 4 * NKT
            emit_qk_mm(0)
            for n in range(NIT):
                pr, kt = divmod(n, NKT)
                sp = n % 2
                pbuf = Pbuf[n % 3]
                if n + 1 < NIT:
                    emit_qk_mm(n + 1)
                P.add("scalar", lambda e, sp=sp, pbuf=pbuf: e.activation(out=pbuf, in_=ps[sp][:].rearrange("p a t -> p (a t)"), func=AF.Exp, scale=0.125),
                      reads=[PSK(2 * sp), PSK(2 * sp + 1)], writes=[("P", n % 3)])

                def pv(e, kt=kt, pbuf=pbuf):
                    e.matmul(bank(4)[0:65, :], lhsT=VA[:, kt, 0, :], rhs=pbuf[:, 0:512], start=(kt == 0), stop=(kt == NKT - 1))
                    return e.matmul(bank(5)[0:65, :], lhsT=VA[:, kt, 1, :], rhs=pbuf[:, 512:1024], start=(kt == 0), stop=(kt == NKT - 1))
                P.add("tensor", pv, reads=[("P", n % 3), ("VA", kt), ("VA1",)], writes=[PSK(4), PSK(5)])
                if kt == NKT - 1:
                    P.add("vector", lambda e: e.tensor_copy(out=Osb[0:65, :, :], in_=ps[2][0:65, :, :]), reads=[PSK(4), PSK(5)], writes=[("Osb",)])
                    emit_o_post(pr)
                yield
            if debug:
                P.add("gpsimd", lambda e: e.dma_start(out=tiled(attn_dbg)[c], in_=attn_tm), reads=[("attn_tm", h_) for h_ in range(8)], dma=("dbg3",))
            yield

        def post_gen(c):
            b = c % 2
            for sub in range(4):
                P.add("scalar", lambda e, sub=sub: e.activation(out=xn[0][:, 0:512], in_=attn_tm[:, sub, :], func=AF.Square, accum_out=st_s),
                      reads=[("attn_tm", h_) for h_ in range(8)], writes=[("xn", 0), ("st_s",)])
                emit_rstd(st_s, st_sm, st_sr, 1, 1.0 / 512, EPS, (("st_s",), ("st_sm",), ("st_sr",)))
                P.add("vector", lambda e, sub=sub: e.scalar_tensor_tensor(out=xn[1][:, 0:512], in0=attn_tm[:, sub, :], scalar=st_sr, in1=g3[:], op0=ALU.mult, op1=ALU.mult),
                      reads=[("attn_tm", h_) for h_ in range(8)] + [("st_sr",), ("g3",)], writes=[("xn", 1)])

                def trm2(e):
                    for k in range(4):
                        i_ = e.transpose(out=bank_bf(6)[:, k * 128:(k + 1) * 128], in_=xn[1][:, k * 128:(k + 1) * 128], identity=ident_b[:])
                    return i_
                P.add("tensor", trm2, reads=[("xn", 1), ("ident_b",)], writes=[PSK(6)])
                P.add("scalar", lambda e, sub=sub: e.activation(out=msT[:, :, sub * 128:(sub + 1) * 128],
                                                                 in_=bank_bf(6)[:, 0:512].rearrange("p (a t) -> p a t", a=4), func=AF.Copy),
                      reads=[PSK(6)], writes=[("msT", sub)])
                yield
            yield from emit_outproj(b, msT, "msT", 0, ybs=(6, 7))
            if debug:
                P.add("gpsimd", lambda e: e.dma_start(out=tiled(x2_dbg)[c], in_=xt[b][:]), reads=[k_ for s_ in range(4) for k_ in xk(b, s_)], dma=("dbg4",))
            yield from emit_norm_T(b, g1, ("g1",), xnT, "xnT", tb=6)
            yield from emit_ffn(b, ("w2_gate", "w2_up", "w2_down"), gbs=(6,), ubs=(7,), ybs=(6, 7), use_tanh=True, split=4, dsplit=4)
            if debug:
                P.add("gpsimd", lambda e: e.dma_start(out=tiled(x3_dbg)[c], in_=xt[b][:]), reads=[k_ for s_ in range(4) for k_ in xk(b, s_)], dma=("dbg5",))
            for sub in range(4):
                P.add("scalar", lambda e, sub=sub: e.activation(out=xn[sub % 2][:], in_=xt[b][:, sub, :], func=AF.Square, accum_out=st_ssq[:, sub:sub + 1]),
                      reads=xk(b, sub), writes=[("xn", sub % 2), ("st_ssq", 0, sub)])
            yield
            P.add("vector", lambda e: e.tensor_scalar(out=st_ms, in0=st_ssq, scalar1=1.0 / D, scalar2=EPS, op0=ALU.mult, op1=ALU.add),
                  reads=[("st_ssq", 0, s_) for s_ in range(4)], writes=[("st_ms", 0)])
            P.add("scalar", lambda e: e.activation(out=st_ms, in_=st_ms, func=AF.Sqrt), reads=[("st_ms", 0)], writes=[("st_ms", 0)])
            P.add("vector", lambda e: e.reciprocal(out=st_rstd, in_=st_ms), reads=[("st_ms", 0)], writes=[("st_rstd", 0)])
            for sub in range(4):
                P.add("vector", lambda e, sub=sub: e.scalar_tensor_tensor(out=xt[b][:, sub, :], in0=xt[b][:, sub, :], scalar=st_rstd[:, sub:sub + 1],
                                                                          in1=g2[:], op0=ALU.mult, op1=ALU.mult),
                      reads=xk(b, sub) + [("st_rstd", 0), ("g2",)], writes=xk(b, sub))
                yield
            P.add("gpsimd", lambda e: e.dma_start(out=tiled(out_d)[c], in_=xt[b][:]),
                  reads=[k_ for s_ in range(4) for k_ in xk(b, s_)], writes=[("out", c)], dma=("xst", b))
            yield

        if n_tiles_b > 0:
            load_q_b(0)
            load_x_b(0)
        prev_post = None
        for c in range(n_tiles_b):
            if c + 1 < n_tiles_b:
                load_q_b(c + 1)
            att = attention_gen(c)
            if prev_post is None:
                drain(att)
            else:
                k = 0
                emitted = 0
                for _ in att:
                    k += 1
                    while emitted * 4 * NKT < k * POST_PIECES:
                        next(prev_post, None)
                        emitted += 1
                    assert k < NKT or emitted >= 4
                drain(prev_post)
            if c + 1 < n_tiles_b:
                load_x_b(c + 1)
            prev_post = post_gen(c)
        if prev_post is not None:
            drain(prev_post)

        P.fence()

        P.finalize(nc, es)
        with nc.Block() as block:
            @block.sync
            def _(e):
                P.replay("sync", e)

            @block.gpsimd
            def _(e):
                P.replay("gpsimd", e)

            @block.scalar
            def _(e):
                P.replay("scalar", e)

            @block.vector
            def _(e):
                P.replay("vector", e)

            @block.tensor
            def _(e):
                P.replay("tensor", e)
    return nc


def rope_table():
    rows = S // 64
    row_idx = np.repeat(np.arange(rows, dtype=np.float32), 64)
    col_idx = np.tile(np.arange(64, dtype=np.float32), rows)
    inv = (1.0 / (np.float32(10000.0) ** (np.arange(0, 32, 2, dtype=np.float32) / np.float32(32)))).astype(np.float32)
    ang = np.concatenate([row_idx[:, None] * inv, col_idx[:, None] * inv], axis=-1).astype(np.float32)
    return np.concatenate([np.cos(ang), np.sin(ang)], axis=-1).astype(np.float32)


def make_in_maps(inputs, n_cores=8):
    x = np.asarray(inputs["x"], dtype=np.float32)
    rope = rope_table()
    shared = {"rope": rope}
    for nm in ("g_ffn1", "g_mix", "g_ffn2", "g_final", "g_q", "g_k", "g_sgu", "g_attn_out", "g_sgu_out", "w_s", "b_s",
               "w1_gate", "w1_up", "w1_down", "w_in", "w_out", "w2_gate", "w2_up", "w2_down"):
        shared[nm] = np.ascontiguousarray(np.asarray(inputs[nm], dtype=np.float32)[0])
    maps = []
    for c in range(n_cores):
        m = dict(shared)
        m["x"] = np.ascontiguousarray(x[c])
        maps.append(m)
    return maps


def kernel(**inputs):
    nc = build_nc()
    in_maps = make_in_maps(inputs)
    res = run_bass_kernel_spmd(nc, in_maps, core_ids=list(range(8)))
    return np.stack([np.asarray(r["out"], dtype=np.float32) for r in res.results], axis=0)
```
